# Optimizing a Trainium2 kernel written in Bass

```python
import jax, jax.numpy as jnp
from jax import lax
import numpy as np

D_MODEL = 2048
BATCH = 4
SEQ = 4096
DEPTH = 4

HEAD_DIM = 128
A_WIDTH = D_MODEL // 2
A_HEADS = A_WIDTH // HEAD_DIM
DILATED_PATTERNS = ((128, 1), (512, 4), (2048, 16))
ATTN_BLOCK = 64
ROPE_THETA = 10000.0
NEG_INF = -1e30
B_WIDTH = D_MODEL // 2
B_CONV = 3
C_WIDTH = D_MODEL
C_GROUPS = 8
C_CHUNK = 128
AB_IN_WIDTH = 4 * A_WIDTH + 4 * B_WIDTH
SG_IN_WIDTH = 3 * C_WIDTH
N_EVEN = (DEPTH + 1) // 2
N_ODD = DEPTH // 2
EPS = 1e-6

kernel_name = 'hybrid_dilated_attn_shortconv_sgu_adaln'


def rms_norm(x, g):
    xf = x.astype(jnp.float32)
    y = xf * lax.rsqrt(jnp.mean(xf * xf, axis=-1, keepdims=True) + EPS)
    return (y * g.astype(jnp.float32)).astype(x.dtype)


def layer_norm(x, g, b):
    xf = x.astype(jnp.float32)
    mu = jnp.mean(xf, axis=-1, keepdims=True)
    xc = xf - mu
    y = xc * lax.rsqrt(jnp.mean(xc * xc, axis=-1, keepdims=True) + EPS)
    return (y * g.astype(jnp.float32) + b.astype(jnp.float32)).astype(x.dtype)


def ada_modulation(c, w_mod, b_mod):
    m = jax.nn.silu(c) @ w_mod + b_mod
    shift, scale, gate = jnp.split(m, 3, axis=-1)
    return shift[:, None, :], scale[:, None, :], gate[:, None, :]


def rope(t, pos):
    half = t.shape[-1] // 2
    inv = ROPE_THETA ** (-jnp.arange(half, dtype=jnp.float32) / half)
    ang = pos[:, None] * inv[None, :]
    cos = jnp.cos(ang)[None, :, None, :]
    sin = jnp.sin(ang)[None, :, None, :]
    tf = t.astype(jnp.float32)
    t1, t2 = tf[..., :half], tf[..., half:]
    out = jnp.concatenate([t1 * cos - t2 * sin, t2 * cos + t1 * sin], axis=-1)
    return out.astype(t.dtype)


def dilated_window_attention(q, k, v, dilation, radius):
    b, h, s, hd = q.shape
    sub_len = s // dilation
    n_blk = -(-sub_len // ATTN_BLOCK)
    lp = n_blk * ATTN_BLOCK
    pad = lp - sub_len

    def to_sub(t):
        return t.reshape(b, h, sub_len, dilation, hd).transpose(0, 1, 3, 2, 4)

    qs = jnp.pad(to_sub(q), ((0, 0), (0, 0), (0, 0), (0, pad), (0, 0)))
    qb = qs.reshape(b, h, dilation, n_blk, ATTN_BLOCK, hd)
    halo = ((0, 0), (0, 0), (0, 0), (ATTN_BLOCK, pad + ATTN_BLOCK), (0, 0))
    kp = jnp.pad(to_sub(k), halo)
    vp = jnp.pad(to_sub(v), halo)

    def band(t):
        return jnp.concatenate(
            [t[:, :, :, o:o + lp].reshape(b, h, dilation, n_blk, ATTN_BLOCK, hd)
             for o in (0, ATTN_BLOCK, 2 * ATTN_BLOCK)], axis=-2)

    kb, vb = band(kp), band(vp)
    blk = jnp.arange(n_blk)[:, None, None] * ATTN_BLOCK
    q_idx = blk + jnp.arange(ATTN_BLOCK)[None, :, None]
    k_idx = blk - ATTN_BLOCK + jnp.arange(3 * ATTN_BLOCK)[None, None, :]
    valid = (jnp.abs(q_idx - k_idx) <= radius) & (k_idx >= 0) & (k_idx < sub_len)

    scores = jnp.einsum('bhrnqd,bhrnkd->bhrnqk', qb, kb,
                        preferred_element_type=jnp.float32) * (hd ** -0.5)
    scores = jnp.where(valid, scores, NEG_INF)
    m = jnp.max(scores, axis=-1, keepdims=True)
    p = jnp.exp(scores - m)
    den = jnp.sum(p, axis=-1, keepdims=True)
    o = jnp.einsum('bhrnqk,bhrnkd->bhrnqd', p, vb.astype(jnp.float32)) / den
    lse = (m + jnp.log(den))[..., 0]
    o = o.reshape(b, h, dilation, lp, hd)[:, :, :, :sub_len]
    o = o.transpose(0, 1, 3, 2, 4).reshape(b, h, s, hd)
    lse = lse.reshape(b, h, dilation, lp)[..., :sub_len].transpose(0, 1, 3, 2).reshape(b, h, s)
    return o, lse


def dilated_mixture_attention(q, k, v):
    outs, lses = [], []
    for window, dilation in DILATED_PATTERNS:
        o, lse = dilated_window_attention(q, k, v, dilation, window // (2 * dilation))
        outs.append(o)
        lses.append(lse)
    w = jax.nn.softmax(jnp.stack(lses, axis=0), axis=0)
    return jnp.einsum('pbhs,pbhsd->bhsd', w, jnp.stack(outs, axis=0))


def short_conv(u, w):
    return lax.conv_general_dilated(
        u, w[:, None, :].astype(u.dtype), window_strides=(1,), padding=((1, 1),),
        dimension_numbers=('NWC', 'WIO', 'NWC'), feature_group_count=u.shape[-1])


def mixer_ab(h, w_in, conv_w, w_out):
    b, s, _ = h.shape
    proj = h @ w_in
    cuts = np.cumsum([A_WIDTH] * 4 + [B_WIDTH] * 3).tolist()
    q, k, v, z_a, u_b, g_b, g_c, z_b = jnp.split(proj, cuts, axis=-1)
    pos = jnp.arange(s, dtype=jnp.float32)
    q = rope(q.reshape(b, s, A_HEADS, HEAD_DIM), pos).transpose(0, 2, 1, 3)
    k = rope(k.reshape(b, s, A_HEADS, HEAD_DIM), pos).transpose(0, 2, 1, 3)
    v = v.reshape(b, s, A_HEADS, HEAD_DIM).transpose(0, 2, 1, 3)
    attn = dilated_mixture_attention(q, k, v)
    y_a = attn.transpose(0, 2, 1, 3).reshape(b, s, A_WIDTH).astype(h.dtype) * jax.nn.silu(z_a)
    y_b = g_b * short_conv(g_c * u_b, conv_w) * jax.nn.silu(z_b)
    return jnp.concatenate([y_a, y_b], axis=-1) @ w_out


def mixer_sgu(h, w_in, ln_g, ln_b, w_s, b_s, w_out):
    b, s, _ = h.shape
    u, v, z = jnp.split(h @ w_in, 3, axis=-1)
    u = jax.nn.gelu(u)
    v = layer_norm(jax.nn.gelu(v), ln_g, ln_b)
    v = v.reshape(b, s // C_CHUNK, C_CHUNK, C_GROUPS, C_WIDTH // C_GROUPS)
    mixed = jnp.einsum('gts,bnsgc->bntgc', w_s, v) + b_s.T[None, None, :, :, None]
    y = u * mixed.reshape(b, s, C_WIDTH) * jax.nn.silu(z)
    return y @ w_out


def setup_inputs(seed: int = 0) -> dict:
    key = jax.random.key(seed)
    ks = jax.random.split(key, 20)
    D = D_MODEL

    def nrm(k, shape, scale):
        return jax.random.normal(k, shape, jnp.float32) * scale

    return {
        'x': nrm(ks[0], (BATCH, SEQ, D), 1.0),
        'c': nrm(ks[1], (BATCH, D), 1.0),
        'ab_norm_g': 1.0 + nrm(ks[2], (N_EVEN, D), 0.02),
        'ab_w_mod': nrm(ks[3], (N_EVEN, D, 3 * D), 0.5 * D ** -0.5),
        'ab_b_mod': nrm(ks[4], (N_EVEN, 3 * D), 0.01),
        'ab_w_in': nrm(ks[5], (N_EVEN, D, AB_IN_WIDTH), D ** -0.5),
        'ab_conv_w': nrm(ks[6], (N_EVEN, B_CONV, B_WIDTH), B_CONV ** -0.5),
        'ab_w_out': nrm(ks[7], (N_EVEN, A_WIDTH + B_WIDTH, D), (A_WIDTH + B_WIDTH) ** -0.5),
        'sg_norm_g': 1.0 + nrm(ks[8], (N_ODD, D), 0.02),
        'sg_w_mod': nrm(ks[9], (N_ODD, D, 3 * D), 0.5 * D ** -0.5),
        'sg_b_mod': nrm(ks[10], (N_ODD, 3 * D), 0.01),
        'sg_w_in': nrm(ks[11], (N_ODD, D, SG_IN_WIDTH), D ** -0.5),
        'sg_ln_g': 1.0 + nrm(ks[12], (N_ODD, C_WIDTH), 0.02),
        'sg_ln_b': nrm(ks[13], (N_ODD, C_WIDTH), 0.01),
        'sg_w_s': nrm(ks[14], (N_ODD, C_GROUPS, C_CHUNK, C_CHUNK), C_CHUNK ** -0.5),
        'sg_b_s': 1.0 + nrm(ks[15], (N_ODD, C_GROUPS, C_CHUNK), 0.01),
        'sg_w_out': nrm(ks[16], (N_ODD, C_WIDTH, D), C_WIDTH ** -0.5),
        'final_norm_g': 1.0 + nrm(ks[17], (D,), 0.02),
    }


def reference(x, c, ab_norm_g, ab_w_mod, ab_b_mod, ab_w_in, ab_conv_w, ab_w_out,
              sg_norm_g, sg_w_mod, sg_b_mod, sg_w_in, sg_ln_g, sg_ln_b, sg_w_s, sg_b_s,
              sg_w_out, final_norm_g):
    for layer in range(DEPTH):
        i = layer // 2
        if layer % 2 == 0:
            shift, scale, gate = ada_modulation(c, ab_w_mod[i], ab_b_mod[i])
            h = rms_norm(x, ab_norm_g[i]) * (1.0 + scale) + shift
            out = mixer_ab(h, ab_w_in[i], ab_conv_w[i], ab_w_out[i])
        else:
            shift, scale, gate = ada_modulation(c, sg_w_mod[i], sg_b_mod[i])
            h = rms_norm(x, sg_norm_g[i]) * (1.0 + scale) + shift
            out = mixer_sgu(h, sg_w_in[i], sg_ln_g[i], sg_ln_b[i], sg_w_s[i], sg_b_s[i], sg_w_out[i])
        x = x + gate * out
    return rms_norm(x, final_norm_g)
```

```python
import contextlib
import os
import numpy as np
import ml_dtypes
import concourse.bass as bass
import concourse.mybir as mybir
from concourse.bass_utils import run_bass_kernel_spmd

F32 = mybir.dt.float32
BF16 = mybir.dt.bfloat16
ALU = mybir.AluOpType
AF = mybir.ActivationFunctionType
ISZ = {F32: 4, BF16: 2}

ENGS = ("sp", "act", "dve", "pool", "pe")
SCELL = 1024
PCELL = 512


class Op:
    __slots__ = ("eng", "fn", "deps", "signaled", "seq", "grp", "gval", "is_dma", "inc", "gneed")

    def __init__(self, eng, fn, is_dma=False, grp=None):
        self.eng = eng
        self.fn = fn
        self.deps = []
        self.signaled = False
        self.seq = 0
        self.grp = grp
        self.gval = 0
        self.is_dma = is_dma
        self.inc = 16
        self.gneed = {}


def ap_cells(ap):
    sp = str(ap.space)
    isz = ISZ[ap.dtype]
    pat = list(ap.ap)
    pstride = abs(pat[0][0]) if pat[0][0] != 0 else (1 << 60)
    off = ap.offset % pstride
    lo = off
    hi = off
    for st, cnt in pat[1:]:
        ext = (cnt - 1) * st
        if ext >= 0:
            hi += ext
        else:
            lo += ext
    lo_b = lo * isz
    hi_b = (hi + 1) * isz
    if "PSUM" in sp.upper():
        tag, cell = "P", PCELL
    else:
        tag, cell = "S", SCELL
    return [(tag, c) for c in range(lo_b // cell, (hi_b - 1) // cell + 1)]


class Prog:
    def __init__(self, nc):
        self.nc = nc
        self.ops = {e: [] for e in ENGS}
        self.state = {}
        self.grp_cnt = {}
        self.nops = 0

    def _lane(self, op):
        return ("g", op.grp) if op.is_dma else ("e", op.eng)

    def _keys(self, aps, keys):
        out = []
        for a in aps:
            out.extend(ap_cells(a))
        out.extend(keys)
        return out

    def _add(self, op, r, w, rk, wk):
        reads = self._keys(r, rk)
        writes = self._keys(w, wk)
        deps = {}
        for k in reads:
            st = self.state.get(k)
            if st is not None and st[0] is not None:
                deps[id(st[0])] = st[0]
        for k in writes:
            st = self.state.get(k)
            if st is not None:
                if st[0] is not None:
                    deps[id(st[0])] = st[0]
                for rd in st[1].values():
                    deps[id(rd)] = rd
        for d in deps.values():
            if d is op:
                continue
            if (not d.is_dma) and (not op.is_dma) and d.eng == "pe" and op.eng == "pe":
                continue
            if d.is_dma:
                op.gneed[d.grp] = self.grp_cnt[d.grp]
            else:
                d.signaled = True
                op.deps.append(d)
        lane = self._lane(op)
        for k in reads:
            st = self.state.get(k)
            if st is None:
                st = self.state[k] = [None, {}]
            st[1][lane] = op
        for k in writes:
            self.state[k] = [op, {}]
        self.ops[op.eng].append(op)
        self.nops += 1
        return op

    def op(self, eng, fn, r=(), w=(), rk=(), wk=()):
        return self._add(Op(eng, fn), r, w, rk, wk)

    def dma(self, eng, grp, out, in_, r=(), w=(), rk=(), wk=(), **kw):
        o = Op(eng, None, is_dma=True, grp=grp)
        o.fn = lambda e: e.dma_start(out=out, in_=in_, **kw)
        res = self._add(o, r, w, rk, wk)
        self.grp_cnt[grp] = self.grp_cnt.get(grp, 0) + 16
        return res

    def async_op(self, eng, grp, fn, inc=1, r=(), w=(), rk=(), wk=()):
        o = Op(eng, fn, is_dma=True, grp=grp)
        o.inc = inc
        res = self._add(o, r, w, rk, wk)
        self.grp_cnt[grp] = self.grp_cnt.get(grp, 0) + inc
        return res

    def emit(self, final_wait_grps=()):
        nc = self.nc
        for e in ENGS:
            c = 0
            for o in self.ops[e]:
                if not o.is_dma and o.signaled:
                    c += 1
                    o.seq = c
        with contextlib.ExitStack() as es:
            esem = {e: es.enter_context(nc.semaphore("s_" + e)) for e in ENGS if e != "sp"}
            gsem = {g: es.enter_context(nc.semaphore("g_%s" % str(g))) for g in self.grp_cnt}
            block = es.enter_context(nc.Block())
            prog = self

            def run(ename, eng):
                known = {}
                for o in prog.ops[ename]:
                    need = {}
                    for g_, v_ in o.gneed.items():
                        need[("g", g_)] = v_
                    for d in o.deps:
                        key = ("e", d.eng)
                        val = d.seq
                        if val > need.get(key, 0):
                            need[key] = val
                    for key, val in need.items():
                        if known.get(key, 0) >= val:
                            continue
                        known[key] = val
                        sem = gsem[key[1]] if key[0] == "g" else esem[key[1]]
                        eng.wait_ge(sem, val)
                    ins = o.fn(eng)
                    if o.is_dma:
                        ins.then_inc(gsem[o.grp], o.inc)
                    elif o.signaled:
                        ins.then_inc(esem[ename], 1)
                if ename == "sp":
                    for g in final_wait_grps:
                        eng.wait_ge(gsem[g], prog.grp_cnt[g])

            @block.sync
            def _(e):
                run("sp", e)

            @block.scalar
            def _(e):
                run("act", e)

            @block.vector
            def _(e):
                run("dve", e)

            @block.gpsimd
            def _(e):
                run("pool", e)

            @block.tensor
            def _(e):
                run("pe", e)

T = 2048
DM = 2048
HD = 128
NH = 8
EPS = 1e-6
SCALE = HD ** -0.5
RG = [[0, 1], [2, 3], [4, 5], [6, 7]]
MASKMM = os.environ.get("ATT_MASKMM", "1") == "1"
LNEXP = os.environ.get("ATT_LNEXP", "1") == "1"
QENG = os.environ.get("ATT_QENG", "pool")

O_BIG = 0
O_W = 8192
O_RA = 16384
O_RB = 24640
O_XT = 32832
O_GSH = 34880
O_CS = 38976
O_HB = 43072
O_STG = 44096
O_TMP = 46144
O_CST = 48192
O_SCT = 48896
O_CW = 49920
O_VLR = 49984
O_BS = 49992
O_STAT = 50008
O_WST = 50136
O_JUNK = 50648
ARENA = 51712

EGROUPS = ([("m", j) for j in range(4)] + [("k", 0), ("k", 1), ("v", 0), ("v", 1),
           ("q", 0), ("q", 1), ("z", 0), ("z", 1)] + [("c", j) for j in range(4)])


def even_perm():
    idx = []
    for kind, j in EGROUPS:
        if kind == "m":
            idx += list(range(4096 + 256 * j, 4096 + 256 * j + 256)) + list(range(6144 + 256 * j, 6144 + 256 * j + 256))
        elif kind == "k":
            idx += list(range(1024 + 512 * j, 1024 + 512 * j + 512))
        elif kind == "v":
            idx += list(range(2048 + 512 * j, 2048 + 512 * j + 512))
        elif kind == "q":
            idx += list(range(512 * j, 512 * j + 512))
        elif kind == "z":
            idx += list(range(3072 + 512 * j, 3072 + 512 * j + 512))
        elif kind == "c":
            idx += list(range(5120 + 256 * j, 5120 + 256 * j + 256)) + list(range(7168 + 256 * j, 7168 + 256 * j + 256))
    return np.array(idx, dtype=np.int64)


class _Stop(Exception):
    pass


def build_nc(nlayers=4, raw=False, stop=None):
    nc = bass.Bass("TRN2", target_bir_lowering=False)

    def din(name, shape, dt=F32):
        return nc.dram_tensor(name, shape, dt, kind="ExternalInput").ap()

    def dint(name, shape, dt=F32):
        return nc.dram_tensor(name, shape, dt, kind="Internal").ap()

    x_in = din("x", [T, DM])
    csil = din("csil", [128, 16, 128])
    NE = max(1, (nlayers + 1) // 2)
    NO = max(1, nlayers // 2)
    w_mod = din("w_mod", [nlayers, DM, 3 * DM])
    b_mod = din("b_mod", [4, 128, 3 * DM])
    norm_g = din("norm_g", [4, 128, DM])
    fin_g = din("fin_g", [128, DM])
    w_in_e = din("w_in_e", [NE, DM, 8192])
    w_in_o = din("w_in_o", [NO, DM, 6144])
    w_out = din("w_out", [nlayers, DM, DM])
    convw = din("convw", [2, 128, 24])
    cos_d = din("cos", [128, T])
    sin_d = din("sin", [128, T])
    cbf_d = din("cbf", [128, 1408], BF16)
    vlr_d = din("vlr", [128, 2])
    wsT_d = din("wsT", [2, 128, 8, 128])
    bs_d = din("bs", [2, 128, 8])
    lng_d = din("lng", [2, 128, DM])
    lnb_d = din("lnb", [2, 128, DM])
    out_d = nc.dram_tensor("out", [T, DM], F32, kind="ExternalOutput").ap()
    xres = dint("xres", [T, DM])
    modb = dint("modb", [4, 3, 128, DM])
    cin_k = [dint("cink%d" % g, [256, 2048], BF16) for g in range(4)]
    cout_k = [dint("coutk%d" % g, [512, 2048], BF16) for g in range(4)]
    cin_v = [dint("cinv%d" % g, [512, 1024], BF16) for g in range(4)]
    cout_v = [dint("coutv%d" % g, [1024, 1024], BF16) for g in range(4)]
    cin_m = dint("cinm", [2, 1024], BF16)
    cout_m = dint("coutm", [4, 1024], BF16)
    v_own = dint("vown", [T, 1024], BF16)
    vL_d = dint("vL", [1024, 1024], BF16)
    vR_d = dint("vR", [1024, 1024], BF16)
    qT_d = dint("qT", [1024, T], BF16)
    szT_d = dint("szT", [1024, T], BF16)


    with contextlib.ExitStack() as es:
        A = es.enter_context(nc.sbuf_tensor("arena", [128, ARENA], F32))
        PS = es.enter_context(nc.psum_tensor("ps", [128, 4096], F32))
        P = Prog(nc)

        def fa(off, n):
            return A[:, off:off + n]

        def ba(off, n):
            return A[:, off:off + n].bitcast(BF16)

        def bank(b):
            return PS[:, 512 * b:512 * (b + 1)]

        def bankb(b):
            return PS[:, 512 * b:512 * (b + 1)].bitcast(BF16)

        def mm(ps, lhsT, rhs, start, stop):
            P.op("pe", lambda e: e.matmul(ps, lhsT=lhsT, rhs=rhs, start=start, stop=stop), r=[lhsT, rhs], w=[ps])

        def tr(ps, in_, ident):
            P.op("pe", lambda e: e.transpose(ps, in_, ident), r=[in_, ident], w=[ps])

        def act(out, in_, func, scale=None, accum=None, bias=None):
            kw = {}
            xr = []
            if scale is not None:
                kw["scale"] = scale
                if not isinstance(scale, (int, float)):
                    xr.append(scale)
            if bias is not None:
                kw["bias"] = bias
                if not isinstance(bias, (int, float)):
                    xr.append(bias)
            w = [out]
            if accum is not None:
                kw["accum_out"] = accum
                w.append(accum)
            P.op("act", lambda e: e.activation(out=out, in_=in_, func=func, **kw), r=[in_] + xr, w=w)

        def tt(eng, out, a, b, op):
            P.op(eng, lambda e: e.tensor_tensor(out=out, in0=a, in1=b, op=op), r=[a, b], w=[out])

        def ts(eng, out, a, s1, s2, op0, op1=None):
            r = [a] + [s for s in (s1, s2) if not isinstance(s, (int, float, type(None)))]
            if op1 is None:
                P.op(eng, lambda e: e.tensor_scalar(out=out, in0=a, scalar1=s1, scalar2=0.0, op0=op0, op1=ALU.add), r=r, w=[out])
            else:
                P.op(eng, lambda e: e.tensor_scalar(out=out, in0=a, scalar1=s1, scalar2=s2, op0=op0, op1=op1), r=r, w=[out])

        def stt(out, a, s, b, op0, op1, xr=()):
            r = [a, b] + ([] if isinstance(s, (int, float)) else [s]) + list(xr)
            P.op("dve", lambda e: e.scalar_tensor_tensor(out=out, in0=a, scalar=s, in1=b, op0=op0, op1=op1), r=r, w=[out])

        def cp(eng, out, in_):
            if eng == "act":
                act(out, in_, AF.Copy)
            else:
                P.op(eng, lambda e: e.tensor_copy(out=out, in_=in_), r=[in_], w=[out])

        def recip(out, in_):
            P.op("dve", lambda e: e.reciprocal(out=out, in_=in_), r=[in_], w=[out])

        Wslot = [ba(O_W + 4096 * s, 4096).rearrange("p (k n) -> p k n", k=16) for s in range(2)]
        XT = fa(O_XT, 2048)
        Gt = fa(O_GSH, 2048)
        SHt = fa(O_GSH + 2048, 2048)
        CSa = fa(O_CS, 2048)
        CSb = fa(O_CS + 2048, 2048)
        HB = ba(O_HB, 1024)
        STG = [ba(O_STG + 512 * s, 512) for s in range(4)]
        STGf = [fa(O_STG + 512 * s, 512) for s in range(4)]
        TMP = [fa(O_TMP + 512 * s, 512) for s in range(4)]
        CB = ba(O_CST, 704)
        ident = CB[:, 0:128]
        ones = CB[:, 128:256]
        pswap = CB[:, 256:384]
        masks = [CB[:, 384 + 256 * k:384 + 256 * (k + 1)] for k in range(4)]
        scT = ba(O_SCT, 1024).rearrange("p (k m) -> p k m", k=16)
        CW = fa(O_CW, 48)
        VLR = fa(O_VLR, 2)
        BSt = fa(O_BS, 16)
        STAT = fa(O_STAT, 128)
        WST = ba(O_WST, 512).rearrange("p (g t) -> p g t", g=8)
        JUNK = ba(O_JUNK, 1024)

        ss = STAT[:, 0:1]
        rstd = STAT[:, 1:2]

        P.dma("sp", "cst", CB, cbf_d, w=[CB])
        P.dma("sp", "cst", CW.rearrange("p (i c) -> p i c", i=2), convw.rearrange("i p c -> p i c"), w=[CW])
        P.dma("sp", "cst", VLR, vlr_d, w=[VLR])
        P.dma("sp", "cst", BSt.rearrange("p (i g) -> p i g", i=2), bs_d.rearrange("i p g -> p i g"), w=[BSt])
        cs_f = fa(O_XT, 2048).rearrange("p (k m) -> p k m", k=16)
        P.dma("sp", "xt0", cs_f, csil, w=[cs_f])
        act(scT, cs_f, AF.Silu)

        cnt = 0
        for l in range(nlayers):
            for n in range(12):
                j, cn = n // 4, n % 4
                Wv = Wslot[cnt % 2]
                P.dma("pool", "W%d" % (cnt % 2), Wv, w_mod[l, :, n * 512:(n + 1) * 512].rearrange("(k p) n -> p k n", p=128), w=[Wv])
                bm = TMP[cnt % 2]
                P.dma("sp", "bm%d" % (cnt % 2), bm, b_mod[l, :, n * 512:(n + 1) * 512], w=[bm])
                ps = bank(cnt % 4)
                for kc in range(16):
                    mm(ps, scT[:, kc, :], Wv[:, kc, :], kc == 0, kc == 15)
                res = TMP[2 + cnt % 2]
                tt("dve", res, ps, bm, ALU.add)
                if j == 1:
                    gb = STGf[cnt % 2]
                    P.dma("sp", "gb%d" % (cnt % 2), gb, norm_g[l, :, cn * 512:(cn + 1) * 512], w=[gb])
                    stt(res, res, 1.0, gb, ALU.add, ALU.mult)
                P.dma("sp", "st_mod", modb[l, j, :, cn * 512:(cn + 1) * 512], res, r=[res], wk=[("modb", l, j, cn)])
                cnt += 1

        def dump_and_stop(tag, fn):
            if stop == tag:
                fn()
                raise _Stop()

        def load_mod(l):
            P.dma("sp", "gsh", Gt, modb[l, 1], rk=[("modb", l, 1, c) for c in range(4)], w=[Gt])
            P.dma("sp", "gsh", SHt, modb[l, 0], rk=[("modb", l, 0, c) for c in range(4)], w=[SHt])

        XTs = [XT, fa(O_TMP, 2048)]
        HBs = [HB, JUNK]
        ssv = [STAT[:, 0:1], STAT[:, 2:3]]
        rsv = [STAT[:, 1:2], STAT[:, 3:4]]

        def phase_A(l, blk, hT):
            xsrc = x_in if l == 0 else xres

            def front(ti):
                tg = blk * 8 + ti
                XTp, HBp, ssp, rsp = XTs[ti % 2], HBs[ti % 2], ssv[ti % 2], rsv[ti % 2]
                P.dma("sp", "xt%d" % (ti % 2), XTp, xsrc[tg * 128:(tg + 1) * 128, :], rk=[("x", tg)], w=[XTp])
                act(HBp, XTp, AF.Square, accum=ssp)
                ts("dve", rsp, ssp, 1.0 / DM, EPS, ALU.mult, ALU.add)
                act(rsp, rsp, AF.Sqrt)
                recip(rsp, rsp)
                stt(XTp, XTp, rsp, Gt, ALU.mult, ALU.mult)
                tt("dve", HBp[:, 0:1024], XTp[:, 0:1024], SHt[:, 0:1024], ALU.add)
                tt("pool", HBp[:, 1024:2048], XTp[:, 1024:2048], SHt[:, 1024:2048], ALU.add)

            def back(ti):
                HBp = HBs[ti % 2]
                for half in range(2):
                    pb = bankb(4 + half + 2 * (ti % 2))
                    for i in range(8):
                        kc = half * 8 + i
                        tr(pb[:, i * 128:(i + 1) * 128], HBp[:, kc * 128:(kc + 1) * 128], ident)
                    dst = hT[:, half * 8:(half + 1) * 8, ti * 128:(ti + 1) * 128]
                    cp("act" if half == 0 else "dve", dst, pb.rearrange("p (a b) -> p a b", a=8))

            front(0)
            for ti in range(8):
                if ti + 1 < 8:
                    front(ti + 1)
                back(ti)

        def phase_D(l, tiles, ysrc, last, GAT, final_blk, FG):
            wout = ba(O_BIG, 16384).rearrange("p (k n) -> p k n", k=16)
            for q4 in range(4):
                wv = wout[:, q4 * 4:(q4 + 1) * 4, :]
                P.dma("pool", "wout", wv, w_out[l, q4 * 512:(q4 + 1) * 512, :].rearrange("(k p) n -> p k n", p=128), w=[wv])
            P.dma("sp", "gat", GAT, modb[l, 2], rk=[("modb", l, 2, c) for c in range(4)], w=[GAT])
            if last and not raw:
                P.dma("sp", "fg", FG, fin_g, w=[FG])
            elif final_blk and not last:
                load_mod(l + 1)
            for it, tg in enumerate(tiles):
                b0 = 4 * (it % 2)
                for kc in range(16):
                    lhsT = ysrc(kc, tg)
                    for n in range(4):
                        mm(bank(b0 + n), lhsT, wout[:, kc, n * 512:(n + 1) * 512], kc == 0, kc == 15)
                P.dma("sp", "xt0", XT, (x_in if l == 0 else xres)[tg * 128:(tg + 1) * 128, :], rk=[("x", tg)], w=[XT])
                for n in range(4):
                    tmp = TMP[n]
                    tt("dve", tmp, bank(b0 + n), GAT[:, n * 512:(n + 1) * 512], ALU.mult)
                    tt("pool", XT[:, n * 512:(n + 1) * 512], XT[:, n * 512:(n + 1) * 512], tmp, ALU.add)
                if last and not raw:
                    act(HB, XT, AF.Square, accum=ss)
                    ts("dve", rstd, ss, 1.0 / DM, EPS, ALU.mult, ALU.add)
                    act(rstd, rstd, AF.Sqrt)
                    recip(rstd, rstd)
                    stt(XT, XT, rstd, FG, ALU.mult, ALU.mult)
                if last:
                    P.dma("sp", "st_out", out_d[tg * 128:(tg + 1) * 128, :], XT, r=[XT], wk=[("out", tg)])
                else:
                    P.dma("sp", "st_x", xres[tg * 128:(tg + 1) * 128, :], XT, r=[XT], wk=[("x", tg)])

        def even_layer(l, last):
            i2 = l // 2
            hT = ba(O_BIG, 8192).rearrange("p (k t) -> p k t", k=16)
            mT = ba(O_RA, 8200).rearrange("p (f t) -> p f t", f=8)
            ybT = ba(O_RB, 8192).rearrange("p (f t) -> p f t", f=8)
            yaT = ba(O_RA, 8192).rearrange("p (f t) -> p f t", f=8)
            cosT, sinT = CSa, CSb
            P.dma("sp", "cs", cosT, cos_d, w=[cosT])
            P.dma("sp", "cs", sinT, sin_d, w=[sinT])
            if l == 0:
                load_mod(l)
            def issue_exchange():
                P.dma("sp", "st_cin", cin_m[0, :].rearrange("(f p) -> p f", p=128), mT[:, :, 1], r=[mT[:, :, 1]], wk=[("cm", 0)],
                      allow_slow_non_contiguous=True)
                P.dma("sp", "st_cin", cin_m[1, :].rearrange("(f p) -> p f", p=128), mT[:, :, 2048], r=[mT[:, :, 2048]], wk=[("cm", 1)],
                      allow_slow_non_contiguous=True)

                def coll(src, dst, rk, wk):
                    P.async_op("pool", "cc", lambda e: e.collective_compute("AllGather", ALU.bypass, replica_groups=RG, ins=[src], outs=[dst]),
                               inc=1, rk=rk, wk=wk)
                coll(cin_m, cout_m, [("cm", 0), ("cm", 1)], ["cout_m"])
                for g in range(4):
                    coll(cin_k[g], cout_k[g], [("ck", h, b) for h in (2 * g, 2 * g + 1) for b in range(2)], [("cout_k", g)])
                for g in range(4):
                    coll(cin_v[g], cout_v[g], [("cv2", tg, j) for tg in range(4 * g, 4 * g + 4) for j in range(2)], [("cout_v", g)])

            wcnt = 0
            stg_i = 0
            pcnt = 0
            for blk in range(2):
                phase_A(l, blk, hT)
                dump_and_stop("A", lambda: [P.dma("pool", "st_out", out_d[kc * 128:(kc + 1) * 128, 0:1024], hT[:, kc, :], r=[hT[:, kc, :]], wk=[("out", kc)]) for kc in range(16)])
                t0 = blk * 1024
                for gi, (kind, j) in enumerate(EGROUPS):
                    Wv = Wslot[wcnt % 2]
                    P.dma("pool", "W%d" % (wcnt % 2), Wv,
                          w_in_e[i2, :, gi * 512:(gi + 1) * 512].rearrange("(k p) n -> p k n", p=128), w=[Wv])
                    wcnt += 1

                    def fm(i, tb):
                        nonlocal pcnt
                        ps = bank(pcnt % 4)
                        pcnt += 1
                        for kc in range(16):
                            mm(ps, Wv[:, kc, i * 128:(i + 1) * 128], hT[:, kc, tb * 512:(tb + 1) * 512], kc == 0, kc == 15)
                        return ps

                    if kind == "m":
                        for tb in range(2):
                            for fl in range(2):
                                ft = 2 * j + fl
                                pa = fm(fl, tb)
                                pb_ = fm(2 + fl, tb)
                                tmp = TMP[pcnt % 4]
                                cp("act", tmp, pa)
                                tt("dve", mT[:, ft, 1 + t0 + tb * 512:1 + t0 + (tb + 1) * 512], pb_, tmp, ALU.mult)
                    elif kind in ("k", "q"):
                        for i in range(4):
                            h = 4 * j + i
                            st = STG[stg_i % 4]
                            stg_i += 1
                            for tb in range(2):
                                ps = fm(i, tb)
                                tok = slice(t0 + tb * 512, t0 + (tb + 1) * 512)
                                qs = ba(O_TMP + 512 * (pcnt % 2), 256)
                                qc = TMP[2 + pcnt % 2]
                                tt("dve", qs, ps, sinT[:, tok], ALU.mult)
                                tt("dve", qc, ps, cosT[:, tok], ALU.mult)
                                ps2 = bank(6 + pcnt % 2)
                                mm(ps2, pswap, qs, True, True)
                                tt("dve", st[:, tb * 512:(tb + 1) * 512], ps2, qc, ALU.add)
                            if kind == "k":
                                P.dma("sp", "st_cin", cin_k[h // 2][(h % 2) * 128:(h % 2 + 1) * 128, t0:t0 + 1024], st, r=[st], wk=[("ck", h, blk)])
                            else:
                                P.dma("sp", "st_q", qT_d[h * 128:(h + 1) * 128, t0:t0 + 1024], st, r=[st], wk=[("q", h, blk)])
                    elif kind == "v":
                        for ti in range(8):
                            tg = blk * 8 + ti
                            ps = bank(pcnt % 4)
                            pcnt += 1
                            for kc in range(16):
                                mm(ps, hT[:, kc, ti * 128:(ti + 1) * 128], Wv[:, kc, :], kc == 0, kc == 15)
                            st = STG[stg_i % 4][:, 0:512]
                            stg_i += 1
                            cp("act", st, ps)
                            P.dma("sp", "st_cin", v_own[tg * 128:(tg + 1) * 128, j * 512:(j + 1) * 512], st, r=[st], wk=[("cv", tg, j)])
                            P.dma("sp", "st_cin", cin_v[tg // 4][(tg % 4) * 128:(tg % 4 + 1) * 128, j * 512:(j + 1) * 512], st, r=[st], wk=[("cv2", tg, j)])
                        if blk == 1 and j == 1:
                            issue_exchange()
                    elif kind == "z":
                        for i in range(4):
                            h = 4 * j + i
                            st = STG[stg_i % 4]
                            stg_i += 1
                            for tb in range(2):
                                ps = fm(i, tb)
                                act(st[:, tb * 512:(tb + 1) * 512], ps, AF.Silu)
                            P.dma("sp", "st_sz", szT_d[h * 128:(h + 1) * 128, t0:t0 + 1024], st, r=[st], wk=[("sz", h, blk)])
                    elif kind == "c":
                        for tb in range(2):
                            for fl in range(2):
                                ft = 2 * j + fl
                                pa = fm(fl, tb)
                                pb_ = fm(2 + fl, tb)
                                tmp = TMP[pcnt % 4]
                                act(tmp, pb_, AF.Silu)
                                tt("dve", ybT[:, ft, t0 + tb * 512:t0 + (tb + 1) * 512], pa, tmp, ALU.mult)
            dump_and_stop("B", lambda: [P.dma("pool", "st_out", out_d[ft * 128:(ft + 1) * 128, 0:2048], mT[:, ft, 1:2049], r=[mT[:, ft, :]], wk=[("out", ft)]) for ft in range(8)]
                          + [P.dma("pool", "st_out", out_d[1024 + ft * 128:1024 + (ft + 1) * 128, 0:2048], ybT[:, ft, :], r=[ybT[:, ft, :]], wk=[("out", 8 + ft)]) for ft in range(8)])
            P.dma("sp", "vh", vL_d[0:512, :], cout_v[2][0:512, :], rk=[("cout_v", 2)], wk=["vL"])
            P.dma("sp", "vh", vL_d[512:1024, :], cout_v[3][0:512, :], rk=[("cout_v", 3)], wk=["vL"])
            P.dma("sp", "vh", vR_d[0:512, :], cout_v[0][512:1024, :], rk=[("cout_v", 0)], wk=["vR"])
            P.dma("sp", "vh", vR_d[512:1024, :], cout_v[1][512:1024, :], rk=[("cout_v", 1)], wk=["vR"])
            P.dma("sp", "mh", mT[:, :, 0], cout_m[1, :].rearrange("(f p) -> p f", p=128), rk=["cout_m"], w=[mT[:, :, 0]],
                  allow_slow_non_contiguous=True)
            P.dma("sp", "mh", mT[:, :, 2049], cout_m[2, :].rearrange("(f p) -> p f", p=128), rk=["cout_m"], w=[mT[:, :, 2049]],
                  allow_slow_non_contiguous=True)
            ts("dve", mT[:, :, 0], mT[:, :, 0], VLR[:, 0:1], None, ALU.mult)
            ts("dve", mT[:, :, 2049], mT[:, :, 2049], VLR[:, 1:2], None, ALU.mult)
            dump_and_stop("X", lambda: [P.dma("pool", "st_out", out_d[ft * 128:(ft + 1) * 128, 0:2048], mT[:, ft, 0:2048], r=[mT[:, ft, :]], wk=[("out", ft)]) for ft in range(8)]
                          + [P.dma("pool", "st_out", out_d[1024:1536, 0:2048], cout_k[0], rk=[("cout_k", 0)], wk=[("out", 8)])])
            dump_and_stop("C", lambda: [P.dma("pool", "st_out", out_d[ft * 128:(ft + 1) * 128, 0:2048], ybT[:, ft, :], r=[ybT[:, ft, :]], wk=[("out", ft)]) for ft in range(8)])
            Vs = [ba(O_BIG + 4416 * s_, 4416).rearrange("p (t c) -> p t c", c=128) for s_ in range(2)]
            PT = [ba(O_BIG + 8832 + 128 * s, 128) for s in range(4)]
            qks = [O_BIG + 9344, O_CS]
            Qs = [ba(o, 1024) for o in qks]
            Ks = [ba(o + 1024, 2048) for o in qks]
            SZs = [ba(o + 3072, 1024) for o in qks]
            acc = fa(O_GSH, 2048)
            den = fa(O_GSH + 2048, 2048)
            PAT = [(1, 0), (4, 17), (16, 37)]
            scnt = 0
            gcnt = 0

            def load_qk(h):
                Qb, Kb, SZb = Qs[h % 2], Ks[h % 2], SZs[h % 2]
                hr = (h % 2) * 128
                P.dma(QENG, "qld%d" % (h % 2), Qb, qT_d[h * 128:(h + 1) * 128, :], rk=[("q", h, 0), ("q", h, 1)], w=[Qb])
                P.dma(QENG, "qld%d" % (h % 2), SZb, szT_d[h * 128:(h + 1) * 128, :], rk=[("sz", h, 0), ("sz", h, 1)], w=[SZb])
                P.dma(QENG, "qld%d" % (h % 2), Kb[:, 0:1024], cout_k[h // 2][hr:hr + 128, 1024:2048], rk=[("cout_k", h // 2)], w=[Kb[:, 0:1024]])
                P.dma(QENG, "qld%d" % (h % 2), Kb[:, 1024:3072], cin_k[h // 2][hr:hr + 128, :], rk=[("ck", h, 0), ("ck", h, 1)], w=[Kb[:, 1024:3072]])
                P.dma(QENG, "qld%d" % (h % 2), Kb[:, 3072:4096], cout_k[h // 2][256 + hr:256 + hr + 128, 0:1024], rk=[("cout_k", h // 2)], w=[Kb[:, 3072:4096]])

            def load_v(h):
                Vb = Vs[h % 2]
                cvk = [("cv", tg, h // 4) for tg in range(16)]
                c0 = h * 128
                for D, base in PAT:
                    Lq = T // D
                    NJ = Lq // 128 + 1
                    Vp = Vb[:, base:base + D * NJ, :].rearrange("p (r j) c -> p r j c", r=D)
                    own = v_own[:, c0:c0 + 128].rearrange("(j a r) c -> a r j c", a=128, r=D)
                    for r in range(D):
                        P.dma("sp", "vld%d" % (h % 2), Vp[64:128, r, 0:NJ - 1, :], own[0:64, r], rk=cvk, w=[Vp[64:128, r, 0:NJ - 1, :]])
                        P.dma("sp", "vld%d" % (h % 2), Vp[0:64, r, 1:NJ, :], own[64:128, r], rk=cvk, w=[Vp[0:64, r, 1:NJ, :]])
                    lsrc = vL_d[1024 - 64 * D:1024, c0:c0 + 128].rearrange("(a r) c -> a r c", r=D)
                    P.dma("sp", "vld%d" % (h % 2), Vp[0:64, :, 0, :], lsrc, rk=["vL"], w=[Vp[0:64, :, 0, :]])
                    rsrc = vR_d[0:64 * D, c0:c0 + 128].rearrange("(a r) c -> a r c", r=D)
                    P.dma("sp", "vld%d" % (h % 2), Vp[64:128, :, NJ - 1, :], rsrc, rk=["vR"], w=[Vp[64:128, :, NJ - 1, :]])

            def conv_ft(ft):
                cw = CW[:, i2 * 24 + ft * 3:i2 * 24 + ft * 3 + 3]
                ts("pool", XT, mT[:, ft, 0:2048], cw[:, 0:1], None, ALU.mult)
                stt(XT, mT[:, ft, 1:2049], cw[:, 1:2], XT, ALU.mult, ALU.add)
                stt(XT, mT[:, ft, 2:2050], cw[:, 2:3], XT, ALU.mult, ALU.add)
                tt("pool", ybT[:, ft, :], XT, ybT[:, ft, :], ALU.mult)

            load_qk(0)
            load_v(0)
            conv_ft(0)
            conv_ft(1)
            for hp in range(4):
                for hh in range(2):
                    h = 2 * hp + hh
                    Qb, Kb, SZb = Qs[h % 2], Ks[h % 2], SZs[h % 2]
                    Vb = Vs[h % 2]
                    if h + 1 < NH:
                        load_qk(h + 1)
                        load_v(h + 1)
                    blocks = []
                    gviews = {}
                    for pi, (D, base) in enumerate(PAT):
                        NB = (T // D) // 128
                        NJ = NB + 1
                        for g in range(4):
                            if D == 1:
                                grp = [(0, 4 * g + b) for b in range(4)]
                                gviews[(pi, g)] = (acc[:, 512 * g:512 * (g + 1)], den[:, 512 * g:512 * (g + 1)])
                            elif D == 4:
                                grp = [(g, b) for b in range(4)]
                                gviews[(pi, g)] = (acc[:, g:2048:4], den[:, g:2048:4])
                            else:
                                grp = [(4 * g + b, 0) for b in range(4)]
                                gviews[(pi, g)] = (acc.rearrange("p (c s) -> p s c", s=16)[:, 4 * g:4 * g + 4, :],
                                                   den.rearrange("p (c s) -> p s c", s=16)[:, 4 * g:4 * g + 4, :])
                            for b, (r, i) in enumerate(grp):
                                blocks.append((pi, D, base, g, b, r, i, NB, NJ))
                    slots = {}
                    gbanks = {}

                    def s_stage(n):
                        nonlocal scnt
                        pi, D, base, g, b, r, i, NB, NJ = blocks[n]
                        sl = scnt % 4
                        scnt += 1
                        slots[n] = sl
                        psc = bank(sl)[:, 0:256]
                        q0 = r + D * 128 * i
                        qc_ = Qb[:, q0:q0 + 127 * D + 1:D]
                        kind = (1 if i == 0 else 0) + (2 if i == NB - 1 else 0)
                        if MASKMM:
                            for jj in range(2):
                                ks = 1024 + r + D * (128 * (i + jj) - 64)
                                mm(psc[:, 128 * jj:128 * (jj + 1)], Kb[:, ks:ks + 127 * D + 1:D], qc_, True, False)
                                mm(psc[:, 128 * jj:128 * (jj + 1)], ident, masks[kind][:, 128 * jj:128 * (jj + 1)], False, True)
                            act(PT[sl], psc, AF.Exp, scale=SCALE)
                        else:
                            for jj in range(2):
                                ks = 1024 + r + D * (128 * (i + jj) - 64)
                                mm(psc[:, 128 * jj:128 * (jj + 1)], Kb[:, ks:ks + 127 * D + 1:D], qc_, True, True)
                            act(PT[sl], psc, AF.Exp, scale=SCALE)
                            tt("pool", PT[sl], PT[sl], masks[kind], ALU.mult)

                    def p_stage(n):
                        nonlocal gcnt
                        pi, D, base, g, b, r, i, NB, NJ = blocks[n]
                        if b == 0:
                            gbanks[(pi, g)] = (bank(4 + gcnt % 2), bank(6 + gcnt % 2))
                            gcnt += 1
                        po, pd = gbanks[(pi, g)]
                        pt = PT[slots[n]]
                        for jj in range(2):
                            vt = Vb[:, base + r * NJ + i + jj, :]
                            mm(po[:, 128 * b:128 * (b + 1)], vt, pt[:, 128 * jj:128 * (jj + 1)], jj == 0, jj == 1)
                        for jj in range(2):
                            mm(pd[:, 128 * b:128 * (b + 1)], ones, pt[:, 128 * jj:128 * (jj + 1)], jj == 0, jj == 1)
                        if b == 3:
                            if D == 16:
                                pov = po.rearrange("p (b c) -> p b c", b=4)
                                pdv = pd.rearrange("p (b c) -> p b c", b=4)
                            else:
                                pov, pdv = po, pd
                            va, vd = gviews[(pi, g)]
                            if pi == 0:
                                cp("dve", va, pov)
                                cp("dve", vd, pdv)
                            else:
                                tt("dve", va, va, pov, ALU.add)
                                tt("dve", vd, vd, pdv, ALU.add)

                    LOOK = int(os.environ.get('ATT_LOOK', '2'))
                    nb = len(blocks)
                    for n in range(min(LOOK, nb)):
                        s_stage(n)
                    for n in range(nb):
                        if LOOK == 0:
                            s_stage(n)
                        elif n + LOOK < nb:
                            s_stage(n + LOOK)
                        p_stage(n)
                    if h + 2 < 8:
                        conv_ft(h + 2)
                    if LNEXP:
                        act(den, den, AF.Ln)
                        act(den, den, AF.Exp, scale=-1.0)
                    else:
                        recip(den, den)
                    tt("dve", acc, acc, den, ALU.mult)
                    tt("pool", yaT[:, h, :], acc, SZb, ALU.mult)

            dump_and_stop("T", lambda: [P.dma("pool", "st_out", out_d[ft * 128:(ft + 1) * 128, 0:2048], yaT[:, ft, :], r=[yaT[:, ft, :]], wk=[("out", ft)]) for ft in range(8)]
                          + [P.dma("pool", "st_out", out_d[1024 + ft * 128:1024 + (ft + 1) * 128, 0:2048], ybT[:, ft, :], r=[ybT[:, ft, :]], wk=[("out", 8 + ft)]) for ft in range(8)])

            def ysrc(kc, tg):
                if kc < 8:
                    return yaT[:, kc, tg * 128:(tg + 1) * 128]
                return ybT[:, kc - 8, tg * 128:(tg + 1) * 128]
            phase_D(l, list(range(16)), ysrc, last, CSa, True, CSb)

        def odd_layer(l, last):
            i2 = l // 2
            hT = ba(O_BIG, 8192).rearrange("p (k t) -> p k t", k=16)
            yT = ba(O_RA, 8192).rearrange("p (k t) -> p k t", k=16)
            gv = ba(O_RB, 8192).rearrange("p (t f) -> p t f", t=8)
            LNG, LNB = CSa, CSb
            P.dma("sp", "cs", LNG, lng_d[i2], w=[LNG])
            P.dma("sp", "cs", LNB, lnb_d[i2], w=[LNB])
            P.dma("pool", "wst", WST, wsT_d[i2], w=[WST])
            wcnt = 0
            pcnt = 0
            s1 = STAT[:, 8:40].rearrange("p (t n) -> p t n", n=4)
            s2 = STAT[:, 40:48]
            mean = STAT[:, 48:56]
            var = STAT[:, 56:64]
            for blk in range(2):
                phase_A(l, blk, hT)
                for n in range(4):
                    Wv = Wslot[wcnt % 2]
                    P.dma("pool", "W%d" % (wcnt % 2), Wv,
                          w_in_o[i2, :, 2048 + n * 512:2048 + (n + 1) * 512].rearrange("(k p) n -> p k n", p=128), w=[Wv])
                    wcnt += 1
                    for ti in range(8):
                        ps = bank(pcnt % 4)
                        pcnt += 1
                        for kc in range(16):
                            mm(ps, hT[:, kc, ti * 128:(ti + 1) * 128], Wv[:, kc, :], kc == 0, kc == 15)
                        act(gv[:, ti, n * 512:(n + 1) * 512], ps, AF.Gelu_apprx_tanh, accum=s1[:, ti, n:n + 1])
                Wu_next = Wslot[wcnt % 2]
                P.dma("pool", "W%d" % (wcnt % 2), Wu_next,
                      w_in_o[i2, :, 0:512].rearrange("(k p) n -> p k n", p=128), w=[Wu_next])
                wcnt += 1
                for ti in range(8):
                    act(HBs[ti % 2], gv[:, ti, :], AF.Square, accum=s2[:, ti:ti + 1])
                P.op("dve", lambda e: e.tensor_reduce(out=mean, in_=s1, axis=mybir.AxisListType.X, op=ALU.add), r=[s1], w=[mean])
                ts("dve", mean, mean, 1.0 / DM, None, ALU.mult)
                tt("dve", var, mean, mean, ALU.mult)
                stt(var, s2, 1.0 / DM, var, ALU.mult, ALU.subtract)
                ts("dve", var, var, EPS, None, ALU.add)
                act(var, var, AF.Sqrt)
                recip(var, var)
                nmr = STAT[:, 64:72]
                stt(nmr, mean, -1.0, var, ALU.mult, ALU.mult)
                for ti in range(8):
                    XL = XTs[ti % 2]
                    act(XL, gv[:, ti, :], AF.Identity, scale=var[:, ti:ti + 1], bias=nmr[:, ti:ti + 1])
                    tt("dve", XL, XL, LNG, ALU.mult)
                    tt("dve", gv[:, ti, 0:1024], XL[:, 0:1024], LNB[:, 0:1024], ALU.add)
                    tt("pool", gv[:, ti, 1024:2048], XL[:, 1024:2048], LNB[:, 1024:2048], ALU.add)
                gu_all = XT.bitcast(BF16).rearrange("p (t f) -> p t f", t=8)
                for c in range(4):
                    Wu = Wu_next
                    Wz = Wslot[wcnt % 2]
                    P.dma("pool", "W%d" % (wcnt % 2), Wz,
                          w_in_o[i2, :, 4096 + c * 512:4096 + (c + 1) * 512].rearrange("(k p) n -> p k n", p=128), w=[Wz])
                    wcnt += 1
                    for ti in range(8):
                        pu = bank(pcnt % 4)
                        pcnt += 1
                        for kc in range(16):
                            mm(pu, hT[:, kc, ti * 128:(ti + 1) * 128], Wu[:, kc, :], kc == 0, kc == 15)
                        act(gu_all[:, ti, :], pu, AF.Gelu_apprx_tanh)
                    if c + 1 < 4:
                        Wu_next = Wslot[wcnt % 2]
                        P.dma("pool", "W%d" % (wcnt % 2), Wu_next,
                              w_in_o[i2, :, (c + 1) * 512:(c + 2) * 512].rearrange("(k p) n -> p k n", p=128), w=[Wu_next])
                        wcnt += 1
                    pend = None
                    for ti in range(8):
                        pz = bank(pcnt % 4)
                        pcnt += 1
                        for kc in range(16):
                            mm(pz, hT[:, kc, ti * 128:(ti + 1) * 128], Wz[:, kc, :], kc == 0, kc == 15)
                        pm = bank(6 + ti % 2)
                        for gg in range(2):
                            g = 2 * c + gg
                            mm(pm[:, gg * 256:(gg + 1) * 256], WST[:, g, :], gv[:, ti, g * 256:(g + 1) * 256], True, True)
                        if pend is not None:
                            pend()
                        sz = TMP[(2 * ti) % 4]
                        tm = TMP[(2 * ti + 1) % 4]
                        act(sz, pz, AF.Silu)
                        for gg in range(2):
                            g = 2 * c + gg
                            stt(tm[:, gg * 256:(gg + 1) * 256], pm[:, gg * 256:(gg + 1) * 256], BSt[:, i2 * 8 + g:i2 * 8 + g + 1],
                                gu_all[:, ti, gg * 256:(gg + 1) * 256], ALU.add, ALU.mult, xr=[pm])
                        yb = HB[:, (ti % 2) * 512:(ti % 2) * 512 + 512]
                        tt("pool", yb, tm, sz, ALU.mult)

                        def pend(ti=ti, yb=yb, c=c):
                            pb = bankb(4 + ti % 2)
                            for i in range(4):
                                tr(pb[:, i * 128:(i + 1) * 128], yb[:, i * 128:(i + 1) * 128], ident)
                            cp("act" if ti % 2 == 0 else "dve", yT[:, 4 * c:4 * c + 4, ti * 128:(ti + 1) * 128],
                               pb[:, 0:512].rearrange("p (a b) -> p a b", a=4))
                    pend()

                def ysrc(kc, tg, blk=blk):
                    tl = tg - blk * 8
                    return yT[:, kc, tl * 128:(tl + 1) * 128]
                phase_D(l, [blk * 8 + ti for ti in range(8)], ysrc, last, fa(O_RB, 2048), blk == 1, fa(O_RB + 2048, 2048))

        try:
            dump_and_stop("mod", lambda: [P.dma("sp", "st_out", out_d[j * 128:(j + 1) * 128, :], modb[0, j], rk=[("modb", 0, j, c) for c in range(4)], wk=[("out", j)]) for j in range(3)])
            for l in range(nlayers):
                last = (l == nlayers - 1)
                if l % 2 == 0:
                    even_layer(l, last)
                else:
                    odd_layer(l, last)
        except _Stop:
            pass
        P.emit(final_wait_grps=["st_out"])
    return nc


def make_inputs(inputs, nlayers=4):
    f32 = np.float32
    x = np.asarray(inputs["x"], f32)
    c = np.asarray(inputs["c"], f32)
    w_mod = np.stack([np.asarray(inputs["ab_w_mod"][0]), np.asarray(inputs["sg_w_mod"][0]),
                      np.asarray(inputs["ab_w_mod"][1]), np.asarray(inputs["sg_w_mod"][1])]).astype(f32)
    b_mod4 = np.stack([inputs["ab_b_mod"][0], inputs["sg_b_mod"][0], inputs["ab_b_mod"][1], inputs["sg_b_mod"][1]]).astype(f32)
    b_mod = np.ascontiguousarray(np.broadcast_to(b_mod4[:, None, :], (4, 128, 3 * DM)))
    ng4 = np.stack([inputs["ab_norm_g"][0], inputs["sg_norm_g"][0], inputs["ab_norm_g"][1], inputs["sg_norm_g"][1]]).astype(f32)
    norm_g = np.ascontiguousarray(np.broadcast_to(ng4[:, None, :], (4, 128, DM)))
    fin_g = np.ascontiguousarray(np.broadcast_to(np.asarray(inputs["final_norm_g"], f32)[None, :], (128, DM)))
    perm = even_perm()
    w_in_e = np.ascontiguousarray(np.asarray(inputs["ab_w_in"], f32)[:, :, perm])
    w_in_o = np.ascontiguousarray(np.asarray(inputs["sg_w_in"], f32))
    w_out = np.stack([inputs["ab_w_out"][0], inputs["sg_w_out"][0], inputs["ab_w_out"][1], inputs["sg_w_out"][1]]).astype(f32)
    cw = np.asarray(inputs["ab_conv_w"], f32)
    convw = np.ascontiguousarray(cw.reshape(2, 3, 8, 128).transpose(0, 3, 2, 1).reshape(2, 128, 24))
    wsT = np.ascontiguousarray(np.asarray(inputs["sg_w_s"], f32).transpose(0, 3, 1, 2))
    bs = np.ascontiguousarray(np.asarray(inputs["sg_b_s"], f32).transpose(0, 2, 1))
    lng = np.ascontiguousarray(np.broadcast_to(np.asarray(inputs["sg_ln_g"], f32)[:, None, :], (2, 128, DM)))
    lnb = np.ascontiguousarray(np.broadcast_to(np.asarray(inputs["sg_ln_b"], f32)[:, None, :], (2, 128, DM)))
    inv = (f32(10000.0) ** (-(np.arange(64, dtype=f32) / f32(64)))).astype(f32)
    invp = inv[np.arange(128) % 64]
    a = np.arange(128)[:, None]
    cc = np.arange(128)[None, :]
    lo = (cc <= a).astype(f32)
    up = (cc >= a).astype(f32)
    psw = np.zeros((128, 128), f32)
    for m in range(128):
        if m < 64:
            psw[m + 64, m] = -1.0
        else:
            psw[m - 64, m] = 1.0
    maps = []
    for core in range(8):
        b, half = core // 2, core % 2
        vL, vR = (1.0, 0.0) if half == 1 else (0.0, 1.0)
        pos = (half * T + np.arange(T)).astype(f32)
        ang = (pos[None, :] * invp[:, None]).astype(f32)
        lo_f = lo.copy()
        lo_f[:64, :] *= vL
        up_l = up.copy()
        up_l[64:, :] *= vR
        NEG = f32(-30000.0)
        mk = [(np.where(m_ > 0.5, f32(0.0), NEG) if MASKMM else m_) for m_ in
              (np.concatenate([lo, up], 1), np.concatenate([lo_f, up], 1),
               np.concatenate([lo, up_l], 1), np.concatenate([lo_f, up_l], 1))]
        cbf = np.concatenate([np.eye(128, dtype=f32), np.ones((128, 128), f32), psw] + mk, 1).astype(ml_dtypes.bfloat16)
        csil = np.ascontiguousarray(np.broadcast_to(c[b].reshape(16, 128).T[:, :, None], (128, 16, 128)))
        maps.append({
            "x": np.ascontiguousarray(x[b, half * T:(half + 1) * T, :]),
            "csil": csil, "w_mod": w_mod[:nlayers], "b_mod": b_mod, "norm_g": norm_g, "fin_g": fin_g,
            "w_in_e": w_in_e[:max(1, (nlayers + 1) // 2)], "w_in_o": w_in_o[:max(1, nlayers // 2)], "w_out": w_out[:nlayers], "convw": convw,
            "cos": np.cos(ang).astype(f32), "sin": np.sin(ang).astype(f32), "cbf": cbf,
            "vlr": np.ascontiguousarray(np.broadcast_to(np.array([vL, vR], f32)[None, :], (128, 2))),
            "wsT": wsT, "bs": bs, "lng": lng, "lnb": lnb,
        })
    return maps


_NC_CACHE = {}


def run(inputs, nlayers=4, raw=False, stop=None):
    if (nlayers, raw, stop) not in _NC_CACHE:
        _NC_CACHE[(nlayers, raw, stop)] = build_nc(nlayers, raw, stop)
    nc = _NC_CACHE[(nlayers, raw, stop)]
    maps = make_inputs(inputs, nlayers)
    res = run_bass_kernel_spmd(nc, maps, core_ids=list(range(8)))
    out = np.empty((4, 4096, DM), np.float32)
    for core in range(8):
        b, half = core // 2, core % 2
        out[b, half * T:(half + 1) * T, :] = res.results[core]["out"]
    return out


def kernel(**inputs):
    return run(inputs, 4)
```

```python
import contextlib
import os
import numpy as np
import ml_dtypes
import concourse.bass as bass
import concourse.mybir as mybir
from concourse.bass_utils import run_bass_kernel_spmd

F32 = mybir.dt.float32
BF16 = mybir.dt.bfloat16
ALU = mybir.AluOpType
AF = mybir.ActivationFunctionType
ISZ = {F32: 4, BF16: 2}

ENGS = ("sp", "act", "dve", "pool", "pe")
SCELL = 1024
PCELL = 512


class Op:
    __slots__ = ("eng", "fn", "deps", "signaled", "seq", "grp", "gval", "is_dma", "inc", "gneed")

    def __init__(self, eng, fn, is_dma=False, grp=None):
        self.eng = eng
        self.fn = fn
        self.deps = []
        self.signaled = False
        self.seq = 0
        self.grp = grp
        self.gval = 0
        self.is_dma = is_dma
        self.inc = 16
        self.gneed = {}


def ap_cells(ap):
    sp = str(ap.space)
    isz = ISZ[ap.dtype]
    pat = list(ap.ap)
    pstride = abs(pat[0][0]) if pat[0][0] != 0 else (1 << 60)
    off = ap.offset % pstride
    lo = off
    hi = off
    for st, cnt in pat[1:]:
        ext = (cnt - 1) * st
        if ext >= 0:
            hi += ext
        else:
            lo += ext
    lo_b = lo * isz
    hi_b = (hi + 1) * isz
    if "PSUM" in sp.upper():
        tag, cell = "P", PCELL
    else:
        tag, cell = "S", SCELL
    return [(tag, c) for c in range(lo_b // cell, (hi_b - 1) // cell + 1)]


class Prog:
    def __init__(self, nc):
        self.nc = nc
        self.ops = {e: [] for e in ENGS}
        self.state = {}
        self.grp_cnt = {}
        self.nops = 0

    def _lane(self, op):
        return ("g", op.grp) if op.is_dma else ("e", op.eng)

    def _keys(self, aps, keys):
        out = []
        for a in aps:
            out.extend(ap_cells(a))
        out.extend(keys)
        return out

    def _add(self, op, r, w, rk, wk):
        reads = self._keys(r, rk)
        writes = self._keys(w, wk)
        deps = {}
        for k in reads:
            st = self.state.get(k)
            if st is not None and st[0] is not None:
                deps[id(st[0])] = st[0]
        for k in writes:
            st = self.state.get(k)
            if st is not None:
                if st[0] is not None:
                    deps[id(st[0])] = st[0]
                for rd in st[1].values():
                    deps[id(rd)] = rd
        for d in deps.values():
            if d is op:
                continue
            if (not d.is_dma) and (not op.is_dma) and d.eng == "pe" and op.eng == "pe":
                continue
            if d.is_dma:
                op.gneed[d.grp] = self.grp_cnt[d.grp]
            else:
                d.signaled = True
                op.deps.append(d)
        lane = self._lane(op)
        for k in reads:
            st = self.state.get(k)
            if st is None:
                st = self.state[k] = [None, {}]
            st[1][lane] = op
        for k in writes:
            self.state[k] = [op, {}]
        self.ops[op.eng].append(op)
        self.nops += 1
        return op

    def op(self, eng, fn, r=(), w=(), rk=(), wk=()):
        return self._add(Op(eng, fn), r, w, rk, wk)

    def dma(self, eng, grp, out, in_, r=(), w=(), rk=(), wk=(), **kw):
        o = Op(eng, None, is_dma=True, grp=grp)
        o.fn = lambda e: e.dma_start(out=out, in_=in_, **kw)
        res = self._add(o, r, w, rk, wk)
        self.grp_cnt[grp] = self.grp_cnt.get(grp, 0) + 16
        return res

    def async_op(self, eng, grp, fn, inc=1, r=(), w=(), rk=(), wk=()):
        o = Op(eng, fn, is_dma=True, grp=grp)
        o.inc = inc
        res = self._add(o, r, w, rk, wk)
        self.grp_cnt[grp] = self.grp_cnt.get(grp, 0) + inc
        return res

    def emit(self, final_wait_grps=()):
        nc = self.nc
        for e in ENGS:
            c = 0
            for o in self.ops[e]:
                if not o.is_dma and o.signaled:
                    c += 1
                    o.seq = c
        with contextlib.ExitStack() as es:
            esem = {e: es.enter_context(nc.semaphore("s_" + e)) for e in ENGS if e != "sp"}
            gsem = {g: es.enter_context(nc.semaphore("g_%s" % str(g))) for g in self.grp_cnt}
            block = es.enter_context(nc.Block())
            prog = self

            def run(ename, eng):
                known = {}
                for o in prog.ops[ename]:
                    need = {}
                    for g_, v_ in o.gneed.items():
                        need[("g", g_)] = v_
                    for d in o.deps:
                        key = ("e", d.eng)
                        val = d.seq
                        if val > need.get(key, 0):
                            need[key] = val
                    for key, val in need.items():
                        if known.get(key, 0) >= val:
                            continue
                        known[key] = val
                        sem = gsem[key[1]] if key[0] == "g" else esem[key[1]]
                        eng.wait_ge(sem, val)
                    ins = o.fn(eng)
                    if o.is_dma:
                        ins.then_inc(gsem[o.grp], o.inc)
                    elif o.signaled:
                        ins.then_inc(esem[ename], 1)
                if ename == "sp":
                    for g in final_wait_grps:
                        eng.wait_ge(gsem[g], prog.grp_cnt[g])

            @block.sync
            def _(e):
                run("sp", e)

            @block.scalar
            def _(e):
                run("act", e)

            @block.vector
            def _(e):
                run("dve", e)

            @block.gpsimd
            def _(e):
                run("pool", e)

            @block.tensor
            def _(e):
                run("pe", e)

T = 2048
DM = 2048
HD = 128
NH = 8
EPS = 1e-6
SCALE = HD ** -0.5
RG = [[0, 1], [2, 3], [4, 5], [6, 7]]
MASKMM = os.environ.get("ATT_MASKMM", "1") == "1"
LNEXP = os.environ.get("ATT_LNEXP", "1") == "1"
QENG = os.environ.get("ATT_QENG", "pool")

O_BIG = 0
O_W = 8192
O_RA = 16384
O_RB = 24640
O_XT = 32832
O_GSH = 34880
O_CS = 38976
O_HB = 43072
O_STG = 44096
O_TMP = 46144
O_CST = 48192
O_SCT = 48896
O_CW = 49920
O_VLR = 49984
O_BS = 49992
O_STAT = 50008
O_WST = 50136
O_JUNK = 50648
ARENA = 51712

EGROUPS = ([("m", j) for j in range(4)] + [("k", 0), ("k", 1), ("v", 0), ("v", 1),
           ("q", 0), ("q", 1), ("z", 0), ("z", 1)] + [("c", j) for j in range(4)])


def even_perm():
    idx = []
    for kind, j in EGROUPS:
        if kind == "m":
            idx += list(range(4096 + 256 * j, 4096 + 256 * j + 256)) + list(range(6144 + 256 * j, 6144 + 256 * j + 256))
        elif kind == "k":
            idx += list(range(1024 + 512 * j, 1024 + 512 * j + 512))
        elif kind == "v":
            idx += list(range(2048 + 512 * j, 2048 + 512 * j + 512))
        elif kind == "q":
            idx += list(range(512 * j, 512 * j + 512))
        elif kind == "z":
            idx += list(range(3072 + 512 * j, 3072 + 512 * j + 512))
        elif kind == "c":
            idx += list(range(5120 + 256 * j, 5120 + 256 * j + 256)) + list(range(7168 + 256 * j, 7168 + 256 * j + 256))
    return np.array(idx, dtype=np.int64)


class _Stop(Exception):
    pass


def build_nc(nlayers=4, raw=False, stop=None):
    nc = bass.Bass("TRN2", target_bir_lowering=False)

    def din(name, shape, dt=F32):
        return nc.dram_tensor(name, shape, dt, kind="ExternalInput").ap()

    def dint(name, shape, dt=F32):
        return nc.dram_tensor(name, shape, dt, kind="Internal").ap()

    x_in = din("x", [T, DM])
    csil = din("csil", [128, 16, 128])
    NE = max(1, (nlayers + 1) // 2)
    NO = max(1, nlayers // 2)
    w_mod = din("w_mod", [nlayers, DM, 3 * DM])
    b_mod = din("b_mod", [4, 128, 3 * DM])
    norm_g = din("norm_g", [4, 128, DM])
    fin_g = din("fin_g", [128, DM])
    w_in_e = din("w_in_e", [NE, DM, 8192])
    w_in_o = din("w_in_o", [NO, DM, 6144])
    w_out = din("w_out", [nlayers, DM, DM])
    convw = din("convw", [2, 128, 24])
    cos_d = din("cos", [128, T])
    sin_d = din("sin", [128, T])
    cbf_d = din("cbf", [128, 1408], BF16)
    vlr_d = din("vlr", [128, 2])
    wsT_d = din("wsT", [2, 128, 8, 128])
    bs_d = din("bs", [2, 128, 8])
    lng_d = din("lng", [2, 128, DM])
    lnb_d = din("lnb", [2, 128, DM])
    out_d = nc.dram_tensor("out", [T, DM], F32, kind="ExternalOutput").ap()
    xres = dint("xres", [T, DM])
    modb = dint("modb", [4, 3, 128, DM])
    cin_k = [dint("cink%d" % g, [256, 2048], BF16) for g in range(4)]
    cout_k = [dint("coutk%d" % g, [512, 2048], BF16) for g in range(4)]
    cin_v = [dint("cinv%d" % g, [512, 1024], BF16) for g in range(4)]
    cout_v = [dint("coutv%d" % g, [1024, 1024], BF16) for g in range(4)]
    cin_m = dint("cinm", [2, 1024], BF16)
    cout_m = dint("coutm", [4, 1024], BF16)
    v_own = dint("vown", [T, 1024], BF16)
    vL_d = dint("vL", [1024, 1024], BF16)
    vR_d = dint("vR", [1024, 1024], BF16)
    qT_d = dint("qT", [1024, T], BF16)
    szT_d = dint("szT", [1024, T], BF16)


    with contextlib.ExitStack() as es:
        A = es.enter_context(nc.sbuf_tensor("arena", [128, ARENA], F32))
        PS = es.enter_context(nc.psum_tensor("ps", [128, 4096], F32))
        P = Prog(nc)

        def fa(off, n):
            return A[:, off:off + n]

        def ba(off, n):
            return A[:, off:off + n].bitcast(BF16)

        def bank(b):
            return PS[:, 512 * b:512 * (b + 1)]

        def bankb(b):
            return PS[:, 512 * b:512 * (b + 1)].bitcast(BF16)

        def mm(ps, lhsT, rhs, start, stop):
            P.op("pe", lambda e: e.matmul(ps, lhsT=lhsT, rhs=rhs, start=start, stop=stop), r=[lhsT, rhs], w=[ps])

        def tr(ps, in_, ident):
            P.op("pe", lambda e: e.transpose(ps, in_, ident), r=[in_, ident], w=[ps])

        def act(out, in_, func, scale=None, accum=None, bias=None):
            kw = {}
            xr = []
            if scale is not None:
                kw["scale"] = scale
                if not isinstance(scale, (int, float)):
                    xr.append(scale)
            if bias is not None:
                kw["bias"] = bias
                if not isinstance(bias, (int, float)):
                    xr.append(bias)
            w = [out]
            if accum is not None:
                kw["accum_out"] = accum
                w.append(accum)
            P.op("act", lambda e: e.activation(out=out, in_=in_, func=func, **kw), r=[in_] + xr, w=w)

        def tt(eng, out, a, b, op):
            P.op(eng, lambda e: e.tensor_tensor(out=out, in0=a, in1=b, op=op), r=[a, b], w=[out])

        def ts(eng, out, a, s1, s2, op0, op1=None):
            r = [a] + [s for s in (s1, s2) if not isinstance(s, (int, float, type(None)))]
            if op1 is None:
                P.op(eng, lambda e: e.tensor_scalar(out=out, in0=a, scalar1=s1, scalar2=0.0, op0=op0, op1=ALU.add), r=r, w=[out])
            else:
                P.op(eng, lambda e: e.tensor_scalar(out=out, in0=a, scalar1=s1, scalar2=s2, op0=op0, op1=op1), r=r, w=[out])

        def stt(out, a, s, b, op0, op1, xr=()):
            r = [a, b] + ([] if isinstance(s, (int, float)) else [s]) + list(xr)
            P.op("dve", lambda e: e.scalar_tensor_tensor(out=out, in0=a, scalar=s, in1=b, op0=op0, op1=op1), r=r, w=[out])

        def cp(eng, out, in_):
            if eng == "act":
                act(out, in_, AF.Copy)
            else:
                P.op(eng, lambda e: e.tensor_copy(out=out, in_=in_), r=[in_], w=[out])

        def recip(out, in_):
            P.op("dve", lambda e: e.reciprocal(out=out, in_=in_), r=[in_], w=[out])

        Wslot = [ba(O_W + 4096 * s, 4096).rearrange("p (k n) -> p k n", k=16) for s in range(2)]
        XT = fa(O_XT, 2048)
        Gt = fa(O_GSH, 2048)
        SHt = fa(O_GSH + 2048, 2048)
        CSa = fa(O_CS, 2048)
        CSb = fa(O_CS + 2048, 2048)
        HB = ba(O_HB, 1024)
        STG = [ba(O_STG + 512 * s, 512) for s in range(4)]
        STGf = [fa(O_STG + 512 * s, 512) for s in range(4)]
        TMP = [fa(O_TMP + 512 * s, 512) for s in range(4)]
        CB = ba(O_CST, 704)
        ident = CB[:, 0:128]
        ones = CB[:, 128:256]
        pswap = CB[:, 256:384]
        masks = [CB[:, 384 + 256 * k:384 + 256 * (k + 1)] for k in range(4)]
        scT = ba(O_SCT, 1024).rearrange("p (k m) -> p k m", k=16)
        CW = fa(O_CW, 48)
        VLR = fa(O_VLR, 2)
        BSt = fa(O_BS, 16)
        STAT = fa(O_STAT, 128)
        WST = ba(O_WST, 512).rearrange("p (g t) -> p g t", g=8)
        JUNK = ba(O_JUNK, 1024)

        ss = STAT[:, 0:1]
        rstd = STAT[:, 1:2]

        P.dma("sp", "cst", CB, cbf_d, w=[CB])
        P.dma("sp", "cst", CW.rearrange("p (i c) -> p i c", i=2), convw.rearrange("i p c -> p i c"), w=[CW])
        P.dma("sp", "cst", VLR, vlr_d, w=[VLR])
        P.dma("sp", "cst", BSt.rearrange("p (i g) -> p i g", i=2), bs_d.rearrange("i p g -> p i g"), w=[BSt])
        cs_f = fa(O_XT, 2048).rearrange("p (k m) -> p k m", k=16)
        P.dma("sp", "xt0", cs_f, csil, w=[cs_f])
        act(scT, cs_f, AF.Silu)

        cnt = 0
        for l in range(nlayers):
            for n in range(12):
                j, cn = n // 4, n % 4
                Wv = Wslot[cnt % 2]
                P.dma("pool", "W%d" % (cnt % 2), Wv, w_mod[l, :, n * 512:(n + 1) * 512].rearrange("(k p) n -> p k n", p=128), w=[Wv])
                bm = TMP[cnt % 2]
                P.dma("sp", "bm%d" % (cnt % 2), bm, b_mod[l, :, n * 512:(n + 1) * 512], w=[bm])
                ps = bank(cnt % 4)
                for kc in range(16):
                    mm(ps, scT[:, kc, :], Wv[:, kc, :], kc == 0, kc == 15)
                res = TMP[2 + cnt % 2]
                tt("dve", res, ps, bm, ALU.add)
                if j == 1:
                    gb = STGf[cnt % 2]
                    P.dma("sp", "gb%d" % (cnt % 2), gb, norm_g[l, :, cn * 512:(cn + 1) * 512], w=[gb])
                    stt(res, res, 1.0, gb, ALU.add, ALU.mult)
                P.dma("sp", "st_mod", modb[l, j, :, cn * 512:(cn + 1) * 512], res, r=[res], wk=[("modb", l, j, cn)])
                cnt += 1

        def dump_and_stop(tag, fn):
            if stop == tag:
                fn()
                raise _Stop()

        def load_mod(l):
            P.dma("sp", "gsh", Gt, modb[l, 1], rk=[("modb", l, 1, c) for c in range(4)], w=[Gt])
            P.dma("sp", "gsh", SHt, modb[l, 0], rk=[("modb", l, 0, c) for c in range(4)], w=[SHt])

        XTs = [XT, fa(O_TMP, 2048)]
        HBs = [HB, JUNK]
        ssv = [STAT[:, 0:1], STAT[:, 2:3]]
        rsv = [STAT[:, 1:2], STAT[:, 3:4]]

        def phase_A(l, blk, hT):
            xsrc = x_in if l == 0 else xres

            def front(ti):
                tg = blk * 8 + ti
                XTp, HBp, ssp, rsp = XTs[ti % 2], HBs[ti % 2], ssv[ti % 2], rsv[ti % 2]
                P.dma("sp", "xt%d" % (ti % 2), XTp, xsrc[tg * 128:(tg + 1) * 128, :], rk=[("x", tg)], w=[XTp])
                act(HBp, XTp, AF.Square, accum=ssp)
                ts("dve", rsp, ssp, 1.0 / DM, EPS, ALU.mult, ALU.add)
                act(rsp, rsp, AF.Sqrt)
                recip(rsp, rsp)
                stt(XTp, XTp, rsp, Gt, ALU.mult, ALU.mult)
                tt("dve", HBp[:, 0:1024], XTp[:, 0:1024], SHt[:, 0:1024], ALU.add)
                tt("pool", HBp[:, 1024:2048], XTp[:, 1024:2048], SHt[:, 1024:2048], ALU.add)

            def back(ti):
                HBp = HBs[ti % 2]
                for half in range(2):
                    pb = bankb(4 + half + 2 * (ti % 2))
                    for i in range(8):
                        kc = half * 8 + i
                        tr(pb[:, i * 128:(i + 1) * 128], HBp[:, kc * 128:(kc + 1) * 128], ident)
                    dst = hT[:, half * 8:(half + 1) * 8, ti * 128:(ti + 1) * 128]
                    cp("act" if half == 0 else "dve", dst, pb.rearrange("p (a b) -> p a b", a=8))

            front(0)
            for ti in range(8):
                if ti + 1 < 8:
                    front(ti + 1)
                back(ti)

        def phase_D(l, tiles, ysrc, last, GAT, final_blk, FG):
            wout = ba(O_BIG, 16384).rearrange("p (k n) -> p k n", k=16)
            for q4 in range(4):
                wv = wout[:, q4 * 4:(q4 + 1) * 4, :]
                P.dma("pool", "wout", wv, w_out[l, q4 * 512:(q4 + 1) * 512, :].rearrange("(k p) n -> p k n", p=128), w=[wv])
            P.dma("sp", "gat", GAT, modb[l, 2], rk=[("modb", l, 2, c) for c in range(4)], w=[GAT])
            if last and not raw:
                P.dma("sp", "fg", FG, fin_g, w=[FG])
            elif final_blk and not last:
                load_mod(l + 1)
            for it, tg in enumerate(tiles):
                b0 = 4 * (it % 2)
                for kc in range(16):
                    lhsT = ysrc(kc, tg)
                    for n in range(4):
                        mm(bank(b0 + n), lhsT, wout[:, kc, n * 512:(n + 1) * 512], kc == 0, kc == 15)
                P.dma("sp", "xt0", XT, (x_in if l == 0 else xres)[tg * 128:(tg + 1) * 128, :], rk=[("x", tg)], w=[XT])
                for n in range(4):
                    tmp = TMP[n]
                    tt("dve", tmp, bank(b0 + n), GAT[:, n * 512:(n + 1) * 512], ALU.mult)
                    tt("pool", XT[:, n * 512:(n + 1) * 512], XT[:, n * 512:(n + 1) * 512], tmp, ALU.add)
                if last and not raw:
                    act(HB, XT, AF.Square, accum=ss)
                    ts("dve", rstd, ss, 1.0 / DM, EPS, ALU.mult, ALU.add)
                    act(rstd, rstd, AF.Sqrt)
                    recip(rstd, rstd)
                    stt(XT, XT, rstd, FG, ALU.mult, ALU.mult)
                if last:
                    P.dma("sp", "st_out", out_d[tg * 128:(tg + 1) * 128, :], XT, r=[XT], wk=[("out", tg)])
                else:
                    P.dma("sp", "st_x", xres[tg * 128:(tg + 1) * 128, :], XT, r=[XT], wk=[("x", tg)])

        def even_layer(l, last):
            i2 = l // 2
            hT = ba(O_BIG, 8192).rearrange("p (k t) -> p k t", k=16)
            mT = ba(O_RA, 8200).rearrange("p (f t) -> p f t", f=8)
            ybT = ba(O_RB, 8192).rearrange("p (f t) -> p f t", f=8)
            yaT = ba(O_RA, 8192).rearrange("p (f t) -> p f t", f=8)
            cosT, sinT = CSa, CSb
            P.dma("sp", "cs", cosT, cos_d, w=[cosT])
            P.dma("sp", "cs", sinT, sin_d, w=[sinT])
            if l == 0:
                load_mod(l)
            def issue_exchange():
                P.dma("sp", "st_cin", cin_m[0, :].rearrange("(f p) -> p f", p=128), mT[:, :, 1], r=[mT[:, :, 1]], wk=[("cm", 0)],
                      allow_slow_non_contiguous=True)
                P.dma("sp", "st_cin", cin_m[1, :].rearrange("(f p) -> p f", p=128), mT[:, :, 2048], r=[mT[:, :, 2048]], wk=[("cm", 1)],
                      allow_slow_non_contiguous=True)

                def coll(src, dst, rk, wk):
                    P.async_op("pool", "cc", lambda e: e.collective_compute("AllGather", ALU.bypass, replica_groups=RG, ins=[src], outs=[dst]),
                               inc=1, rk=rk, wk=wk)
                coll(cin_m, cout_m, [("cm", 0), ("cm", 1)], ["cout_m"])
                for g in range(4):
                    coll(cin_k[g], cout_k[g], [("ck", h, b) for h in (2 * g, 2 * g + 1) for b in range(2)], [("cout_k", g)])
                for g in range(4):
                    coll(cin_v[g], cout_v[g], [("cv2", tg, j) for tg in range(4 * g, 4 * g + 4) for j in range(2)], [("cout_v", g)])

            wcnt = 0
            stg_i = 0
            pcnt = 0
            for blk in range(2):
                phase_A(l, blk, hT)
                dump_and_stop("A", lambda: [P.dma("pool", "st_out", out_d[kc * 128:(kc + 1) * 128, 0:1024], hT[:, kc, :], r=[hT[:, kc, :]], wk=[("out", kc)]) for kc in range(16)])
                t0 = blk * 1024
                for gi, (kind, j) in enumerate(EGROUPS):
                    Wv = Wslot[wcnt % 2]
                    P.dma("pool", "W%d" % (wcnt % 2), Wv,
                          w_in_e[i2, :, gi * 512:(gi + 1) * 512].rearrange("(k p) n -> p k n", p=128), w=[Wv])
                    wcnt += 1

                    def fm(i, tb):
                        nonlocal pcnt
                        ps = bank(pcnt % 4)
                        pcnt += 1
                        for kc in range(16):
                            mm(ps, Wv[:, kc, i * 128:(i + 1) * 128], hT[:, kc, tb * 512:(tb + 1) * 512], kc == 0, kc == 15)
                        return ps

                    if kind == "m":
                        for tb in range(2):
                            for fl in range(2):
                                ft = 2 * j + fl
                                pa = fm(fl, tb)
                                pb_ = fm(2 + fl, tb)
                                tmp = TMP[pcnt % 4]
                                cp("act", tmp, pa)
                                tt("dve", mT[:, ft, 1 + t0 + tb * 512:1 + t0 + (tb + 1) * 512], pb_, tmp, ALU.mult)
                    elif kind in ("k", "q"):
                        for i in range(4):
                            h = 4 * j + i
                            st = STG[stg_i % 4]
                            stg_i += 1
                            for tb in range(2):
                                ps = fm(i, tb)
                                tok = slice(t0 + tb * 512, t0 + (tb + 1) * 512)
                                qs = ba(O_TMP + 512 * (pcnt % 2), 256)
                                qc = TMP[2 + pcnt % 2]
                                tt("dve", qs, ps, sinT[:, tok], ALU.mult)
                                tt("dve", qc, ps, cosT[:, tok], ALU.mult)
                                ps2 = bank(6 + pcnt % 2)
                                mm(ps2, pswap, qs, True, True)
                                tt("dve", st[:, tb * 512:(tb + 1) * 512], ps2, qc, ALU.add)
                            if kind == "k":
                                P.dma("sp", "st_cin", cin_k[h // 2][(h % 2) * 128:(h % 2 + 1) * 128, t0:t0 + 1024], st, r=[st], wk=[("ck", h, blk)])
                            else:
                                P.dma("sp", "st_q", qT_d[h * 128:(h + 1) * 128, t0:t0 + 1024], st, r=[st], wk=[("q", h, blk)])
                    elif kind == "v":
                        for ti in range(8):
                            tg = blk * 8 + ti
                            ps = bank(pcnt % 4)
                            pcnt += 1
                            for kc in range(16):
                                mm(ps, hT[:, kc, ti * 128:(ti + 1) * 128], Wv[:, kc, :], kc == 0, kc == 15)
                            st = STG[stg_i % 4][:, 0:512]
                            stg_i += 1
                            cp("act", st, ps)
                            P.dma("sp", "st_cin", v_own[tg * 128:(tg + 1) * 128, j * 512:(j + 1) * 512], st, r=[st], wk=[("cv", tg, j)])
                            P.dma("sp", "st_cin", cin_v[tg // 4][(tg % 4) * 128:(tg % 4 + 1) * 128, j * 512:(j + 1) * 512], st, r=[st], wk=[("cv2", tg, j)])
                        if blk == 1 and j == 1:
                            issue_exchange()
                    elif kind == "z":
                        for i in range(4):
                            h = 4 * j + i
                            st = STG[stg_i % 4]
                            stg_i += 1
                            for tb in range(2):
                                ps = fm(i, tb)
                                act(st[:, tb * 512:(tb + 1) * 512], ps, AF.Silu)
                            P.dma("sp", "st_sz", szT_d[h * 128:(h + 1) * 128, t0:t0 + 1024], st, r=[st], wk=[("sz", h, blk)])
                    elif kind == "c":
                        for tb in range(2):
                            for fl in range(2):
                                ft = 2 * j + fl
                                pa = fm(fl, tb)
                                pb_ = fm(2 + fl, tb)
                                tmp = TMP[pcnt % 4]
                                act(tmp, pb_, AF.Silu)
                                tt("dve", ybT[:, ft, t0 + tb * 512:t0 + (tb + 1) * 512], pa, tmp, ALU.mult)
            dump_and_stop("B", lambda: [P.dma("pool", "st_out", out_d[ft * 128:(ft + 1) * 128, 0:2048], mT[:, ft, 1:2049], r=[mT[:, ft, :]], wk=[("out", ft)]) for ft in range(8)]
                          + [P.dma("pool", "st_out", out_d[1024 + ft * 128:1024 + (ft + 1) * 128, 0:2048], ybT[:, ft, :], r=[ybT[:, ft, :]], wk=[("out", 8 + ft)]) for ft in range(8)])
            P.dma("sp", "vh", vL_d[0:512, :], cout_v[2][0:512, :], rk=[("cout_v", 2)], wk=["vL"])
            P.dma("sp", "vh", vL_d[512:1024, :], cout_v[3][0:512, :], rk=[("cout_v", 3)], wk=["vL"])
            P.dma("sp", "vh", vR_d[0:512, :], cout_v[0][512:1024, :], rk=[("cout_v", 0)], wk=["vR"])
            P.dma("sp", "vh", vR_d[512:1024, :], cout_v[1][512:1024, :], rk=[("cout_v", 1)], wk=["vR"])
            P.dma("sp", "mh", mT[:, :, 0], cout_m[1, :].rearrange("(f p) -> p f", p=128), rk=["cout_m"], w=[mT[:, :, 0]],
                  allow_slow_non_contiguous=True)
            P.dma("sp", "mh", mT[:, :, 2049], cout_m[2, :].rearrange("(f p) -> p f", p=128), rk=["cout_m"], w=[mT[:, :, 2049]],
                  allow_slow_non_contiguous=True)
            ts("dve", mT[:, :, 0], mT[:, :, 0], VLR[:, 0:1], None, ALU.mult)
            ts("dve", mT[:, :, 2049], mT[:, :, 2049], VLR[:, 1:2], None, ALU.mult)
            dump_and_stop("X", lambda: [P.dma("pool", "st_out", out_d[ft * 128:(ft + 1) * 128, 0:2048], mT[:, ft, 0:2048], r=[mT[:, ft, :]], wk=[("out", ft)]) for ft in range(8)]
                          + [P.dma("pool", "st_out", out_d[1024:1536, 0:2048], cout_k[0], rk=[("cout_k", 0)], wk=[("out", 8)])])
            dump_and_stop("C", lambda: [P.dma("pool", "st_out", out_d[ft * 128:(ft + 1) * 128, 0:2048], ybT[:, ft, :], r=[ybT[:, ft, :]], wk=[("out", ft)]) for ft in range(8)])
            Vs = [ba(O_BIG + 4416 * s_, 4416).rearrange("p (t c) -> p t c", c=128) for s_ in range(2)]
            PT = [ba(O_BIG + 8832 + 128 * s, 128) for s in range(4)]
            qks = [O_BIG + 9344, O_CS]
            Qs = [ba(o, 1024) for o in qks]
            Ks = [ba(o + 1024, 2048) for o in qks]
            SZs = [ba(o + 3072, 1024) for o in qks]
            acc = fa(O_GSH, 2048)
            den = fa(O_GSH + 2048, 2048)
            PAT = [(1, 0), (4, 17), (16, 37)]
            scnt = 0
            gcnt = 0

            def load_qk(h):
                Qb, Kb, SZb = Qs[h % 2], Ks[h % 2], SZs[h % 2]
                hr = (h % 2) * 128
                P.dma(QENG, "qld%d" % (h % 2), Qb, qT_d[h * 128:(h + 1) * 128, :], rk=[("q", h, 0), ("q", h, 1)], w=[Qb])
                P.dma(QENG, "qld%d" % (h % 2), SZb, szT_d[h * 128:(h + 1) * 128, :], rk=[("sz", h, 0), ("sz", h, 1)], w=[SZb])
                P.dma(QENG, "qld%d" % (h % 2), Kb[:, 0:1024], cout_k[h // 2][hr:hr + 128, 1024:2048], rk=[("cout_k", h // 2)], w=[Kb[:, 0:1024]])
                P.dma(QENG, "qld%d" % (h % 2), Kb[:, 1024:3072], cin_k[h // 2][hr:hr + 128, :], rk=[("ck", h, 0), ("ck", h, 1)], w=[Kb[:, 1024:3072]])
                P.dma(QENG, "qld%d" % (h % 2), Kb[:, 3072:4096], cout_k[h // 2][256 + hr:256 + hr + 128, 0:1024], rk=[("cout_k", h // 2)], w=[Kb[:, 3072:4096]])

            def load_v(h):
                Vb = Vs[h % 2]
                cvk = [("cv", tg, h // 4) for tg in range(16)]
                c0 = h * 128
                for D, base in PAT:
                    Lq = T // D
                    NJ = Lq // 128 + 1
                    Vp = Vb[:, base:base + D * NJ, :].rearrange("p (r j) c -> p r j c", r=D)
                    own = v_own[:, c0:c0 + 128].rearrange("(j a r) c -> a r j c", a=128, r=D)
                    if NJ == 2:
                        P.dma("sp", "vld%d" % (h % 2), Vp[64:128, :, 0, :], own[0:64, :, 0, :], rk=cvk, w=[Vp[64:128, :, 0, :]])
                        P.dma("sp", "vld%d" % (h % 2), Vp[0:64, :, 1, :], own[64:128, :, 0, :], rk=cvk, w=[Vp[0:64, :, 1, :]])
                    else:
                        for r in range(D):
                            P.dma("sp", "vld%d" % (h % 2), Vp[64:128, r, 0:NJ - 1, :], own[0:64, r], rk=cvk, w=[Vp[64:128, r, 0:NJ - 1, :]])
                            P.dma("sp", "vld%d" % (h % 2), Vp[0:64, r, 1:NJ, :], own[64:128, r], rk=cvk, w=[Vp[0:64, r, 1:NJ, :]])
                    lsrc = vL_d[1024 - 64 * D:1024, c0:c0 + 128].rearrange("(a r) c -> a r c", r=D)
                    P.dma("sp", "vld%d" % (h % 2), Vp[0:64, :, 0, :], lsrc, rk=["vL"], w=[Vp[0:64, :, 0, :]])
                    rsrc = vR_d[0:64 * D, c0:c0 + 128].rearrange("(a r) c -> a r c", r=D)
                    P.dma("sp", "vld%d" % (h % 2), Vp[64:128, :, NJ - 1, :], rsrc, rk=["vR"], w=[Vp[64:128, :, NJ - 1, :]])

            def conv_ft(ft):
                cw = CW[:, i2 * 24 + ft * 3:i2 * 24 + ft * 3 + 3]
                ts("pool", XT, mT[:, ft, 0:2048], cw[:, 0:1], None, ALU.mult)
                stt(XT, mT[:, ft, 1:2049], cw[:, 1:2], XT, ALU.mult, ALU.add)
                stt(XT, mT[:, ft, 2:2050], cw[:, 2:3], XT, ALU.mult, ALU.add)
                tt("pool", ybT[:, ft, :], XT, ybT[:, ft, :], ALU.mult)

            load_qk(0)
            load_v(0)
            conv_ft(0)
            conv_ft(1)
            for hp in range(4):
                for hh in range(2):
                    h = 2 * hp + hh
                    Qb, Kb, SZb = Qs[h % 2], Ks[h % 2], SZs[h % 2]
                    Vb = Vs[h % 2]
                    if h + 1 < NH:
                        load_qk(h + 1)
                        load_v(h + 1)
                    blocks = []
                    gviews = {}
                    for pi, (D, base) in enumerate(PAT):
                        NB = (T // D) // 128
                        NJ = NB + 1
                        for g in range(4):
                            if D == 1:
                                grp = [(0, 4 * g + b) for b in range(4)]
                                gviews[(pi, g)] = (acc[:, 512 * g:512 * (g + 1)], den[:, 512 * g:512 * (g + 1)])
                            elif D == 4:
                                grp = [(g, b) for b in range(4)]
                                gviews[(pi, g)] = (acc[:, g:2048:4], den[:, g:2048:4])
                            else:
                                grp = [(4 * g + b, 0) for b in range(4)]
                                gviews[(pi, g)] = (acc.rearrange("p (c s) -> p s c", s=16)[:, 4 * g:4 * g + 4, :],
                                                   den.rearrange("p (c s) -> p s c", s=16)[:, 4 * g:4 * g + 4, :])
                            for b, (r, i) in enumerate(grp):
                                blocks.append((pi, D, base, g, b, r, i, NB, NJ))
                    slots = {}
                    gbanks = {}

                    def s_stage(n):
                        nonlocal scnt
                        pi, D, base, g, b, r, i, NB, NJ = blocks[n]
                        sl = scnt % 4
                        scnt += 1
                        slots[n] = sl
                        psc = bank(sl)[:, 0:256]
                        q0 = r + D * 128 * i
                        qc_ = Qb[:, q0:q0 + 127 * D + 1:D]
                        kind = (1 if i == 0 else 0) + (2 if i == NB - 1 else 0)
                        if MASKMM:
                            for jj in range(2):
                                ks = 1024 + r + D * (128 * (i + jj) - 64)
                                mm(psc[:, 128 * jj:128 * (jj + 1)], Kb[:, ks:ks + 127 * D + 1:D], qc_, True, False)
                                mm(psc[:, 128 * jj:128 * (jj + 1)], ident, masks[kind][:, 128 * jj:128 * (jj + 1)], False, True)
                            act(PT[sl], psc, AF.Exp, scale=SCALE)
                        else:
                            for jj in range(2):
                                ks = 1024 + r + D * (128 * (i + jj) - 64)
                                mm(psc[:, 128 * jj:128 * (jj + 1)], Kb[:, ks:ks + 127 * D + 1:D], qc_, True, True)
                            act(PT[sl], psc, AF.Exp, scale=SCALE)
                            tt("pool", PT[sl], PT[sl], masks[kind], ALU.mult)

                    def p_stage(n):
                        nonlocal gcnt
                        pi, D, base, g, b, r, i, NB, NJ = blocks[n]
                        if b == 0:
                            gbanks[(pi, g)] = (bank(4 + gcnt % 2), bank(6 + gcnt % 2))
                            gcnt += 1
                        po, pd = gbanks[(pi, g)]
                        pt = PT[slots[n]]
                        for jj in range(2):
                            vt = Vb[:, base + r * NJ + i + jj, :]
                            mm(po[:, 128 * b:128 * (b + 1)], vt, pt[:, 128 * jj:128 * (jj + 1)], jj == 0, jj == 1)
                        for jj in range(2):
                            mm(pd[:, 128 * b:128 * (b + 1)], ones, pt[:, 128 * jj:128 * (jj + 1)], jj == 0, jj == 1)
                        if b == 3:
                            if D == 16:
                                pov = po.rearrange("p (b c) -> p b c", b=4)
                                pdv = pd.rearrange("p (b c) -> p b c", b=4)
                            else:
                                pov, pdv = po, pd
                            va, vd = gviews[(pi, g)]
                            if pi == 0:
                                cp("dve", va, pov)
                                cp("dve", vd, pdv)
                            else:
                                tt("dve", va, va, pov, ALU.add)
                                tt("dve", vd, vd, pdv, ALU.add)

                    LOOK = int(os.environ.get('ATT_LOOK', '2'))
                    nb = len(blocks)
                    for n in range(min(LOOK, nb)):
                        s_stage(n)
                    for n in range(nb):
                        if LOOK == 0:
                            s_stage(n)
                        elif n + LOOK < nb:
                            s_stage(n + LOOK)
                        p_stage(n)
                    if h + 2 < 8:
                        conv_ft(h + 2)
                    if LNEXP:
                        act(den, den, AF.Ln)
                        act(den, den, AF.Exp, scale=-1.0)
                    else:
                        recip(den, den)
                    tt("dve", acc, acc, den, ALU.mult)
                    tt("pool", yaT[:, h, :], acc, SZb, ALU.mult)

            dump_and_stop("T", lambda: [P.dma("pool", "st_out", out_d[ft * 128:(ft + 1) * 128, 0:2048], yaT[:, ft, :], r=[yaT[:, ft, :]], wk=[("out", ft)]) for ft in range(8)]
                          + [P.dma("pool", "st_out", out_d[1024 + ft * 128:1024 + (ft + 1) * 128, 0:2048], ybT[:, ft, :], r=[ybT[:, ft, :]], wk=[("out", 8 + ft)]) for ft in range(8)])

            def ysrc(kc, tg):
                if kc < 8:
                    return yaT[:, kc, tg * 128:(tg + 1) * 128]
                return ybT[:, kc - 8, tg * 128:(tg + 1) * 128]
            phase_D(l, list(range(16)), ysrc, last, CSa, True, CSb)

        def odd_layer(l, last):
            i2 = l // 2
            hT = ba(O_BIG, 8192).rearrange("p (k t) -> p k t", k=16)
            yT = ba(O_RA, 8192).rearrange("p (k t) -> p k t", k=16)
            gv = ba(O_RB, 8192).rearrange("p (t f) -> p t f", t=8)
            LNG, LNB = CSa, CSb
            P.dma("sp", "cs", LNG, lng_d[i2], w=[LNG])
            P.dma("sp", "cs", LNB, lnb_d[i2], w=[LNB])
            P.dma("pool", "wst", WST, wsT_d[i2], w=[WST])
            wcnt = 0
            pcnt = 0
            s1 = STAT[:, 8:40].rearrange("p (t n) -> p t n", n=4)
            s2 = STAT[:, 40:48]
            mean = STAT[:, 48:56]
            var = STAT[:, 56:64]
            for blk in range(2):
                phase_A(l, blk, hT)
                for n in range(4):
                    Wv = Wslot[wcnt % 2]
                    P.dma("pool", "W%d" % (wcnt % 2), Wv,
                          w_in_o[i2, :, 2048 + n * 512:2048 + (n + 1) * 512].rearrange("(k p) n -> p k n", p=128), w=[Wv])
                    wcnt += 1
                    for ti in range(8):
                        ps = bank(pcnt % 4)
                        pcnt += 1
                        for kc in range(16):
                            mm(ps, hT[:, kc, ti * 128:(ti + 1) * 128], Wv[:, kc, :], kc == 0, kc == 15)
                        act(gv[:, ti, n * 512:(n + 1) * 512], ps, AF.Gelu_apprx_tanh, accum=s1[:, ti, n:n + 1])
                Wu_next = Wslot[wcnt % 2]
                P.dma("pool", "W%d" % (wcnt % 2), Wu_next,
                      w_in_o[i2, :, 0:512].rearrange("(k p) n -> p k n", p=128), w=[Wu_next])
                wcnt += 1
                for ti in range(8):
                    act(HBs[ti % 2], gv[:, ti, :], AF.Square, accum=s2[:, ti:ti + 1])
                P.op("dve", lambda e: e.tensor_reduce(out=mean, in_=s1, axis=mybir.AxisListType.X, op=ALU.add), r=[s1], w=[mean])
                ts("dve", mean, mean, 1.0 / DM, None, ALU.mult)
                tt("dve", var, mean, mean, ALU.mult)
                stt(var, s2, 1.0 / DM, var, ALU.mult, ALU.subtract)
                ts("dve", var, var, EPS, None, ALU.add)
                act(var, var, AF.Sqrt)
                recip(var, var)
                nmr = STAT[:, 64:72]
                stt(nmr, mean, -1.0, var, ALU.mult, ALU.mult)
                for ti in range(8):
                    XL = XTs[ti % 2]
                    act(XL, gv[:, ti, :], AF.Identity, scale=var[:, ti:ti + 1], bias=nmr[:, ti:ti + 1])
                    tt("dve", XL, XL, LNG, ALU.mult)
                    tt("dve", gv[:, ti, 0:1024], XL[:, 0:1024], LNB[:, 0:1024], ALU.add)
                    tt("pool", gv[:, ti, 1024:2048], XL[:, 1024:2048], LNB[:, 1024:2048], ALU.add)
                gu_all = XT.bitcast(BF16).rearrange("p (t f) -> p t f", t=8)
                for c in range(4):
                    Wu = Wu_next
                    Wz = Wslot[wcnt % 2]
                    P.dma("pool", "W%d" % (wcnt % 2), Wz,
                          w_in_o[i2, :, 4096 + c * 512:4096 + (c + 1) * 512].rearrange("(k p) n -> p k n", p=128), w=[Wz])
                    wcnt += 1
                    for ti in range(8):
                        pu = bank(pcnt % 4)
                        pcnt += 1
                        for kc in range(16):
                            mm(pu, hT[:, kc, ti * 128:(ti + 1) * 128], Wu[:, kc, :], kc == 0, kc == 15)
                        act(gu_all[:, ti, :], pu, AF.Gelu_apprx_tanh)
                    if c + 1 < 4:
                        Wu_next = Wslot[wcnt % 2]
                        P.dma("pool", "W%d" % (wcnt % 2), Wu_next,
                              w_in_o[i2, :, (c + 1) * 512:(c + 2) * 512].rearrange("(k p) n -> p k n", p=128), w=[Wu_next])
                        wcnt += 1
                    pend = None
                    for ti in range(8):
                        pz = bank(pcnt % 4)
                        pcnt += 1
                        for kc in range(16):
                            mm(pz, hT[:, kc, ti * 128:(ti + 1) * 128], Wz[:, kc, :], kc == 0, kc == 15)
                        pm = bank(6 + ti % 2)
                        for gg in range(2):
                            g = 2 * c + gg
                            mm(pm[:, gg * 256:(gg + 1) * 256], WST[:, g, :], gv[:, ti, g * 256:(g + 1) * 256], True, True)
                        if pend is not None:
                            pend()
                        sz = TMP[(2 * ti) % 4]
                        tm = TMP[(2 * ti + 1) % 4]
                        act(sz, pz, AF.Silu)
                        for gg in range(2):
                            g = 2 * c + gg
                            stt(tm[:, gg * 256:(gg + 1) * 256], pm[:, gg * 256:(gg + 1) * 256], BSt[:, i2 * 8 + g:i2 * 8 + g + 1],
                                gu_all[:, ti, gg * 256:(gg + 1) * 256], ALU.add, ALU.mult, xr=[pm])
                        yb = HB[:, (ti % 2) * 512:(ti % 2) * 512 + 512]
                        tt("pool", yb, tm, sz, ALU.mult)

                        def pend(ti=ti, yb=yb, c=c):
                            pb = bankb(4 + ti % 2)
                            for i in range(4):
                                tr(pb[:, i * 128:(i + 1) * 128], yb[:, i * 128:(i + 1) * 128], ident)
                            cp("act" if ti % 2 == 0 else "dve", yT[:, 4 * c:4 * c + 4, ti * 128:(ti + 1) * 128],
                               pb[:, 0:512].rearrange("p (a b) -> p a b", a=4))
                    pend()

                def ysrc(kc, tg, blk=blk):
                    tl = tg - blk * 8
                    return yT[:, kc, tl * 128:(tl + 1) * 128]
                phase_D(l, [blk * 8 + ti for ti in range(8)], ysrc, last, fa(O_RB, 2048), blk == 1, fa(O_RB + 2048, 2048))

        try:
            dump_and_stop("mod", lambda: [P.dma("sp", "st_out", out_d[j * 128:(j + 1) * 128, :], modb[0, j], rk=[("modb", 0, j, c) for c in range(4)], wk=[("out", j)]) for j in range(3)])
            for l in range(nlayers):
                last = (l == nlayers - 1)
                if l % 2 == 0:
                    even_layer(l, last)
                else:
                    odd_layer(l, last)
        except _Stop:
            pass
        P.emit(final_wait_grps=["st_out"])
    return nc


def make_inputs(inputs, nlayers=4):
    f32 = np.float32
    x = np.asarray(inputs["x"], f32)
    c = np.asarray(inputs["c"], f32)
    w_mod = np.stack([np.asarray(inputs["ab_w_mod"][0]), np.asarray(inputs["sg_w_mod"][0]),
                      np.asarray(inputs["ab_w_mod"][1]), np.asarray(inputs["sg_w_mod"][1])]).astype(f32)
    b_mod4 = np.stack([inputs["ab_b_mod"][0], inputs["sg_b_mod"][0], inputs["ab_b_mod"][1], inputs["sg_b_mod"][1]]).astype(f32)
    b_mod = np.ascontiguousarray(np.broadcast_to(b_mod4[:, None, :], (4, 128, 3 * DM)))
    ng4 = np.stack([inputs["ab_norm_g"][0], inputs["sg_norm_g"][0], inputs["ab_norm_g"][1], inputs["sg_norm_g"][1]]).astype(f32)
    norm_g = np.ascontiguousarray(np.broadcast_to(ng4[:, None, :], (4, 128, DM)))
    fin_g = np.ascontiguousarray(np.broadcast_to(np.asarray(inputs["final_norm_g"], f32)[None, :], (128, DM)))
    perm = even_perm()
    w_in_e = np.ascontiguousarray(np.asarray(inputs["ab_w_in"], f32)[:, :, perm])
    w_in_o = np.ascontiguousarray(np.asarray(inputs["sg_w_in"], f32))
    w_out = np.stack([inputs["ab_w_out"][0], inputs["sg_w_out"][0], inputs["ab_w_out"][1], inputs["sg_w_out"][1]]).astype(f32)
    cw = np.asarray(inputs["ab_conv_w"], f32)
    convw = np.ascontiguousarray(cw.reshape(2, 3, 8, 128).transpose(0, 3, 2, 1).reshape(2, 128, 24))
    wsT = np.ascontiguousarray(np.asarray(inputs["sg_w_s"], f32).transpose(0, 3, 1, 2))
    bs = np.ascontiguousarray(np.asarray(inputs["sg_b_s"], f32).transpose(0, 2, 1))
    lng = np.ascontiguousarray(np.broadcast_to(np.asarray(inputs["sg_ln_g"], f32)[:, None, :], (2, 128, DM)))
    lnb = np.ascontiguousarray(np.broadcast_to(np.asarray(inputs["sg_ln_b"], f32)[:, None, :], (2, 128, DM)))
    inv = (f32(10000.0) ** (-(np.arange(64, dtype=f32) / f32(64)))).astype(f32)
    invp = inv[np.arange(128) % 64]
    a = np.arange(128)[:, None]
    cc = np.arange(128)[None, :]
    lo = (cc <= a).astype(f32)
    up = (cc >= a).astype(f32)
    psw = np.zeros((128, 128), f32)
    for m in range(128):
        if m < 64:
            psw[m + 64, m] = -1.0
        else:
            psw[m - 64, m] = 1.0
    maps = []
    for core in range(8):
        b, half = core // 2, core % 2
        vL, vR = (1.0, 0.0) if half == 1 else (0.0, 1.0)
        pos = (half * T + np.arange(T)).astype(f32)
        ang = (pos[None, :] * invp[:, None]).astype(f32)
        lo_f = lo.copy()
        lo_f[:64, :] *= vL
        up_l = up.copy()
        up_l[64:, :] *= vR
        NEG = f32(-30000.0)
        mk = [(np.where(m_ > 0.5, f32(0.0), NEG) if MASKMM else m_) for m_ in
              (np.concatenate([lo, up], 1), np.concatenate([lo_f, up], 1),
               np.concatenate([lo, up_l], 1), np.concatenate([lo_f, up_l], 1))]
        cbf = np.concatenate([np.eye(128, dtype=f32), np.ones((128, 128), f32), psw] + mk, 1).astype(ml_dtypes.bfloat16)
        csil = np.ascontiguousarray(np.broadcast_to(c[b].reshape(16, 128).T[:, :, None], (128, 16, 128)))
        maps.append({
            "x": np.ascontiguousarray(x[b, half * T:(half + 1) * T, :]),
            "csil": csil, "w_mod": w_mod[:nlayers], "b_mod": b_mod, "norm_g": norm_g, "fin_g": fin_g,
            "w_in_e": w_in_e[:max(1, (nlayers + 1) // 2)], "w_in_o": w_in_o[:max(1, nlayers // 2)], "w_out": w_out[:nlayers], "convw": convw,
            "cos": np.cos(ang).astype(f32), "sin": np.sin(ang).astype(f32), "cbf": cbf,
            "vlr": np.ascontiguousarray(np.broadcast_to(np.array([vL, vR], f32)[None, :], (128, 2))),
            "wsT": wsT, "bs": bs, "lng": lng, "lnb": lnb,
        })
    return maps


_NC_CACHE = {}


def run(inputs, nlayers=4, raw=False, stop=None):
    if (nlayers, raw, stop) not in _NC_CACHE:
        _NC_CACHE[(nlayers, raw, stop)] = build_nc(nlayers, raw, stop)
    nc = _NC_CACHE[(nlayers, raw, stop)]
    maps = make_inputs(inputs, nlayers)
    res = run_bass_kernel_spmd(nc, maps, core_ids=list(range(8)))
    out = np.empty((4, 4096, DM), np.float32)
    for core in range(8):
        b, half = core // 2, core % 2
        out[b, half * T:(half + 1) * T, :] = res.results[core]["out"]
    return out


def kernel(**inputs):
    return run(inputs, 4)
```

```python
import contextlib
import os
import numpy as np
import ml_dtypes
import concourse.bass as bass
import concourse.mybir as mybir
from concourse.bass_utils import run_bass_kernel_spmd

F32 = mybir.dt.float32
BF16 = mybir.dt.bfloat16
ALU = mybir.AluOpType
AF = mybir.ActivationFunctionType
ISZ = {F32: 4, BF16: 2}

ENGS = ("sp", "act", "dve", "pool", "pe")
SCELL = 1024
PCELL = 512


class Op:
    __slots__ = ("eng", "fn", "deps", "signaled", "seq", "grp", "gval", "is_dma", "inc", "gneed")

    def __init__(self, eng, fn, is_dma=False, grp=None):
        self.eng = eng
        self.fn = fn
        self.deps = []
        self.signaled = False
        self.seq = 0
        self.grp = grp
        self.gval = 0
        self.is_dma = is_dma
        self.inc = 16
        self.gneed = {}


def ap_cells(ap):
    sp = str(ap.space)
    isz = ISZ[ap.dtype]
    pat = list(ap.ap)
    pstride = abs(pat[0][0]) if pat[0][0] != 0 else (1 << 60)
    off = ap.offset % pstride
    lo = off
    hi = off
    for st, cnt in pat[1:]:
        ext = (cnt - 1) * st
        if ext >= 0:
            hi += ext
        else:
            lo += ext
    lo_b = lo * isz
    hi_b = (hi + 1) * isz
    if "PSUM" in sp.upper():
        tag, cell = "P", PCELL
    else:
        tag, cell = "S", SCELL
    return [(tag, c) for c in range(lo_b // cell, (hi_b - 1) // cell + 1)]


class Prog:
    def __init__(self, nc):
        self.nc = nc
        self.ops = {e: [] for e in ENGS}
        self.state = {}
        self.grp_cnt = {}
        self.nops = 0

    def _lane(self, op):
        return ("g", op.grp) if op.is_dma else ("e", op.eng)

    def _keys(self, aps, keys):
        out = []
        for a in aps:
            out.extend(ap_cells(a))
        out.extend(keys)
        return out

    def _add(self, op, r, w, rk, wk):
        reads = self._keys(r, rk)
        writes = self._keys(w, wk)
        deps = {}
        for k in reads:
            st = self.state.get(k)
            if st is not None and st[0] is not None:
                deps[id(st[0])] = st[0]
        for k in writes:
            st = self.state.get(k)
            if st is not None:
                if st[0] is not None:
                    deps[id(st[0])] = st[0]
                for rd in st[1].values():
                    deps[id(rd)] = rd
        for d in deps.values():
            if d is op:
                continue
            if (not d.is_dma) and (not op.is_dma) and d.eng == "pe" and op.eng == "pe":
                continue
            if d.is_dma:
                op.gneed[d.grp] = self.grp_cnt[d.grp]
            else:
                d.signaled = True
                op.deps.append(d)
        lane = self._lane(op)
        for k in reads:
            st = self.state.get(k)
            if st is None:
                st = self.state[k] = [None, {}]
            st[1][lane] = op
        for k in writes:
            self.state[k] = [op, {}]
        self.ops[op.eng].append(op)
        self.nops += 1
        return op

    def op(self, eng, fn, r=(), w=(), rk=(), wk=()):
        return self._add(Op(eng, fn), r, w, rk, wk)

    def dma(self, eng, grp, out, in_, r=(), w=(), rk=(), wk=(), **kw):
        o = Op(eng, None, is_dma=True, grp=grp)
        o.fn = lambda e: e.dma_start(out=out, in_=in_, **kw)
        res = self._add(o, r, w, rk, wk)
        self.grp_cnt[grp] = self.grp_cnt.get(grp, 0) + 16
        return res

    def async_op(self, eng, grp, fn, inc=1, r=(), w=(), rk=(), wk=()):
        o = Op(eng, fn, is_dma=True, grp=grp)
        o.inc = inc
        res = self._add(o, r, w, rk, wk)
        self.grp_cnt[grp] = self.grp_cnt.get(grp, 0) + inc
        return res

    def emit(self, final_wait_grps=()):
        nc = self.nc
        for e in ENGS:
            c = 0
            for o in self.ops[e]:
                if not o.is_dma and o.signaled:
                    c += 1
                    o.seq = c
        with contextlib.ExitStack() as es:
            esem = {e: es.enter_context(nc.semaphore("s_" + e)) for e in ENGS if e != "sp"}
            gsem = {g: es.enter_context(nc.semaphore("g_%s" % str(g))) for g in self.grp_cnt}
            block = es.enter_context(nc.Block())
            prog = self

            def run(ename, eng):
                known = {}
                for o in prog.ops[ename]:
                    need = {}
                    for g_, v_ in o.gneed.items():
                        need[("g", g_)] = v_
                    for d in o.deps:
                        key = ("e", d.eng)
                        val = d.seq
                        if val > need.get(key, 0):
                            need[key] = val
                    for key, val in need.items():
                        if known.get(key, 0) >= val:
                            continue
                        known[key] = val
                        sem = gsem[key[1]] if key[0] == "g" else esem[key[1]]
                        eng.wait_ge(sem, val)
                    ins = o.fn(eng)
                    if o.is_dma:
                        ins.then_inc(gsem[o.grp], o.inc)
                    elif o.signaled:
                        ins.then_inc(esem[ename], 1)
                if ename == "sp":
                    for g in final_wait_grps:
                        eng.wait_ge(gsem[g], prog.grp_cnt[g])

            @block.sync
            def _(e):
                run("sp", e)

            @block.scalar
            def _(e):
                run("act", e)

            @block.vector
            def _(e):
                run("dve", e)

            @block.gpsimd
            def _(e):
                run("pool", e)

            @block.tensor
            def _(e):
                run("pe", e)

T = 2048
DM = 2048
HD = 128
NH = 8
EPS = 1e-6
SCALE = HD ** -0.5
RG = [[0, 1], [2, 3], [4, 5], [6, 7]]
MASKMM = os.environ.get("ATT_MASKMM", "1") == "1"
LNEXP = os.environ.get("ATT_LNEXP", "1") == "1"
QENG = os.environ.get("ATT_QENG", "pool")

O_BIG = 0
O_W = 8192
O_RA = 16384
O_RB = 24640
O_XT = 32832
O_GSH = 34880
O_CS = 38976
O_HB = 43072
O_STG = 44096
O_TMP = 46144
O_CST = 48192
O_SCT = 48896
O_CW = 49920
O_VLR = 49984
O_BS = 49992
O_STAT = 50008
O_WST = 50136
O_JUNK = 50648
ARENA = 51712

EGROUPS = ([("m", j) for j in range(4)] + [("k", 0), ("k", 1), ("v", 0), ("v", 1),
           ("q", 0), ("q", 1), ("z", 0), ("z", 1)] + [("c", j) for j in range(4)])


def even_perm():
    idx = []
    for kind, j in EGROUPS:
        if kind == "m":
            idx += list(range(4096 + 256 * j, 4096 + 256 * j + 256)) + list(range(6144 + 256 * j, 6144 + 256 * j + 256))
        elif kind == "k":
            idx += list(range(1024 + 512 * j, 1024 + 512 * j + 512))
        elif kind == "v":
            idx += list(range(2048 + 512 * j, 2048 + 512 * j + 512))
        elif kind == "q":
            idx += list(range(512 * j, 512 * j + 512))
        elif kind == "z":
            idx += list(range(3072 + 512 * j, 3072 + 512 * j + 512))
        elif kind == "c":
            idx += list(range(5120 + 256 * j, 5120 + 256 * j + 256)) + list(range(7168 + 256 * j, 7168 + 256 * j + 256))
    return np.array(idx, dtype=np.int64)


class _Stop(Exception):
    pass


def build_nc(nlayers=4, raw=False, stop=None):
    nc = bass.Bass("TRN2", target_bir_lowering=False)

    def din(name, shape, dt=F32):
        return nc.dram_tensor(name, shape, dt, kind="ExternalInput").ap()

    def dint(name, shape, dt=F32):
        return nc.dram_tensor(name, shape, dt, kind="Internal").ap()

    x_in = din("x", [T, DM])
    csil = din("csil", [128, 16, 4])
    sel_d = din("sel", [4, 128])
    NE = max(1, (nlayers + 1) // 2)
    NO = max(1, nlayers // 2)
    w_mod = din("w_mod", [nlayers, DM, 3072])
    cin_mod = dint("cinmod", [4, nlayers * 3072])
    cout_mod = dint("coutmod", [8, nlayers * 3072])
    b_mod = din("b_mod", [4, 128, 3 * DM])
    norm_g = din("norm_g", [4, 128, DM])
    fin_g = din("fin_g", [128, DM])
    w_in_e = din("w_in_e", [NE, DM, 8192])
    w_in_o = din("w_in_o", [NO, DM, 6144])
    w_out = din("w_out", [nlayers, DM, DM])
    convw = din("convw", [2, 128, 24])
    cos_d = din("cos", [128, T])
    sin_d = din("sin", [128, T])
    cbf_d = din("cbf", [128, 1408], BF16)
    vlr_d = din("vlr", [128, 2])
    wsT_d = din("wsT", [2, 128, 8, 128])
    bs_d = din("bs", [2, 128, 8])
    lng_d = din("lng", [2, 128, DM])
    lnb_d = din("lnb", [2, 128, DM])
    out_d = nc.dram_tensor("out", [T, DM], F32, kind="ExternalOutput").ap()
    xres = dint("xres", [T, DM])
    modb = dint("modb", [4, 3, 128, DM])
    cin_k = [dint("cink%d" % g, [256, 2048], BF16) for g in range(4)]
    cout_k = [dint("coutk%d" % g, [512, 2048], BF16) for g in range(4)]
    cin_v = [dint("cinv%d" % g, [512, 1024], BF16) for g in range(4)]
    cout_v = [dint("coutv%d" % g, [1024, 1024], BF16) for g in range(4)]
    cin_m = dint("cinm", [2, 1024], BF16)
    cout_m = dint("coutm", [4, 1024], BF16)
    v_own = dint("vown", [T, 1024], BF16)
    vL_d = dint("vL", [1024, 1024], BF16)
    vR_d = dint("vR", [1024, 1024], BF16)
    qT_d = dint("qT", [1024, T], BF16)
    szT_d = dint("szT", [1024, T], BF16)


    with contextlib.ExitStack() as es:
        A = es.enter_context(nc.sbuf_tensor("arena", [128, ARENA], F32))
        PS = es.enter_context(nc.psum_tensor("ps", [128, 4096], F32))
        P = Prog(nc)

        def fa(off, n):
            return A[:, off:off + n]

        def ba(off, n):
            return A[:, off:off + n].bitcast(BF16)

        def bank(b):
            return PS[:, 512 * b:512 * (b + 1)]

        def bankb(b):
            return PS[:, 512 * b:512 * (b + 1)].bitcast(BF16)

        def mm(ps, lhsT, rhs, start, stop):
            P.op("pe", lambda e: e.matmul(ps, lhsT=lhsT, rhs=rhs, start=start, stop=stop), r=[lhsT, rhs], w=[ps])

        def tr(ps, in_, ident):
            P.op("pe", lambda e: e.transpose(ps, in_, ident), r=[in_, ident], w=[ps])

        def act(out, in_, func, scale=None, accum=None, bias=None):
            kw = {}
            xr = []
            if scale is not None:
                kw["scale"] = scale
                if not isinstance(scale, (int, float)):
                    xr.append(scale)
            if bias is not None:
                kw["bias"] = bias
                if not isinstance(bias, (int, float)):
                    xr.append(bias)
            w = [out]
            if accum is not None:
                kw["accum_out"] = accum
                w.append(accum)
            P.op("act", lambda e: e.activation(out=out, in_=in_, func=func, **kw), r=[in_] + xr, w=w)

        def tt(eng, out, a, b, op):
            P.op(eng, lambda e: e.tensor_tensor(out=out, in0=a, in1=b, op=op), r=[a, b], w=[out])

        def ts(eng, out, a, s1, s2, op0, op1=None):
            r = [a] + [s for s in (s1, s2) if not isinstance(s, (int, float, type(None)))]
            if op1 is None:
                P.op(eng, lambda e: e.tensor_scalar(out=out, in0=a, scalar1=s1, scalar2=0.0, op0=op0, op1=ALU.add), r=r, w=[out])
            else:
                P.op(eng, lambda e: e.tensor_scalar(out=out, in0=a, scalar1=s1, scalar2=s2, op0=op0, op1=op1), r=r, w=[out])

        def stt(out, a, s, b, op0, op1, xr=()):
            r = [a, b] + ([] if isinstance(s, (int, float)) else [s]) + list(xr)
            P.op("dve", lambda e: e.scalar_tensor_tensor(out=out, in0=a, scalar=s, in1=b, op0=op0, op1=op1), r=r, w=[out])

        def cp(eng, out, in_):
            if eng == "act":
                act(out, in_, AF.Copy)
            else:
                P.op(eng, lambda e: e.tensor_copy(out=out, in_=in_), r=[in_], w=[out])

        def recip(out, in_):
            P.op("dve", lambda e: e.reciprocal(out=out, in_=in_), r=[in_], w=[out])

        Wslot = [ba(O_W + 4096 * s, 4096).rearrange("p (k n) -> p k n", k=16) for s in range(2)]
        XT = fa(O_XT, 2048)
        Gt = fa(O_GSH, 2048)
        SHt = fa(O_GSH + 2048, 2048)
        CSa = fa(O_CS, 2048)
        CSb = fa(O_CS + 2048, 2048)
        HB = ba(O_HB, 1024)
        STG = [ba(O_STG + 512 * s, 512) for s in range(4)]
        STGf = [fa(O_STG + 512 * s, 512) for s in range(4)]
        TMP = [fa(O_TMP + 512 * s, 512) for s in range(4)]
        CB = ba(O_CST, 704)
        ident = CB[:, 0:128]
        ones = CB[:, 128:256]
        pswap = CB[:, 256:384]
        masks = [CB[:, 384 + 256 * k:384 + 256 * (k + 1)] for k in range(4)]
        scT = ba(O_SCT, 1024).rearrange("p (k m) -> p k m", k=16)
        CW = fa(O_CW, 48)
        VLR = fa(O_VLR, 2)
        BSt = fa(O_BS, 16)
        STAT = fa(O_STAT, 128)
        WST = ba(O_WST, 512).rearrange("p (g t) -> p g t", g=8)
        JUNK = ba(O_JUNK, 1024)

        ss = STAT[:, 0:1]
        rstd = STAT[:, 1:2]

        P.dma("sp", "cst", CB, cbf_d, w=[CB])
        P.dma("sp", "cst", CW.rearrange("p (i c) -> p i c", i=2), convw.rearrange("i p c -> p i c"), w=[CW])
        P.dma("sp", "cst", VLR, vlr_d, w=[VLR])
        P.dma("sp", "cst", BSt.rearrange("p (i g) -> p i g", i=2), bs_d.rearrange("i p g -> p i g"), w=[BSt])
        cs_f = fa(O_XT, 64).rearrange("p (k m) -> p k m", k=16)
        scT4 = ba(O_SCT, 32).rearrange("p (k m) -> p k m", k=16)
        SEL = fa(O_SCT + 64, 128)[0:4, :]
        rows4 = fa(O_RA, nlayers * 3072)[0:4, :]
        P.dma("sp", "xt0", cs_f, csil, w=[cs_f])
        P.dma("sp", "cst", SEL, sel_d, w=[SEL])
        act(scT4, cs_f, AF.Silu)

        cnt = 0
        for l in range(nlayers):
            for c0, wd in [(c_, 512) for c_ in range(0, 3072, 512)]:
                Wv = Wslot[cnt % 2][:, :, 0:wd]
                P.dma("pool", "W%d" % (cnt % 2), Wv, w_mod[l, :, c0:c0 + wd].rearrange("(k p) n -> p k n", p=128), w=[Wv])
                ps = bank(cnt % 4)[0:4, 0:wd]
                for kc in range(16):
                    mm(ps, scT4[:, kc, :], Wv[:, kc, :], kc == 0, kc == 15)
                cp("dve", rows4[:, l * 3072 + c0:l * 3072 + c0 + wd], ps)
                cnt += 1
        P.dma("sp", "st_mod", cin_mod, rows4, r=[rows4], wk=["cinmod"])
        P.async_op("pool", "cc", lambda e: e.collective_compute("AllGather", ALU.bypass, replica_groups=RG,
                                                                ins=[cin_mod], outs=[cout_mod]),
                   inc=1, rk=["cinmod"], wk=["coutmod"])
        for l in range(nlayers):
            g_l = fa(O_BIG, 6144)[0:4, :]
            P.dma("sp", "gl", g_l.rearrange("b (r c) -> b r c", r=2), cout_mod[:, l * 3072:(l + 1) * 3072].rearrange("(r b) c -> b r c", b=4),
                  rk=["coutmod"], w=[g_l])
            for n in range(12):
                j, cn = n // 4, n % 4
                bm = TMP[cnt % 2]
                P.dma("sp", "bm%d" % (cnt % 2), bm, b_mod[l, :, n * 512:(n + 1) * 512], w=[bm])
                ps = bank(cnt % 4)
                mm(ps, SEL, g_l[:, n * 512:(n + 1) * 512], True, True)
                res = TMP[2 + cnt % 2]
                tt("dve", res, ps, bm, ALU.add)
                if j == 1:
                    gb = STGf[cnt % 2]
                    P.dma("sp", "gb%d" % (cnt % 2), gb, norm_g[l, :, cn * 512:(cn + 1) * 512], w=[gb])
                    stt(res, res, 1.0, gb, ALU.add, ALU.mult)
                P.dma("sp", "st_mod", modb[l, j, :, cn * 512:(cn + 1) * 512], res, r=[res], wk=[("modb", l, j, cn)])
                cnt += 1

        def dump_and_stop(tag, fn):
            if stop == tag:
                fn()
                raise _Stop()

        def load_mod(l):
            P.dma("sp", "gsh", Gt, modb[l, 1], rk=[("modb", l, 1, c) for c in range(4)], w=[Gt])
            P.dma("sp", "gsh", SHt, modb[l, 0], rk=[("modb", l, 0, c) for c in range(4)], w=[SHt])

        XTs = [XT, fa(O_TMP, 2048)]
        HBs = [HB, JUNK]
        ssv = [STAT[:, 0:1], STAT[:, 2:3]]
        rsv = [STAT[:, 1:2], STAT[:, 3:4]]

        def phase_A(l, blk, hT):
            xsrc = x_in if l == 0 else xres

            def front(ti):
                tg = blk * 8 + ti
                XTp, HBp, ssp, rsp = XTs[ti % 2], HBs[ti % 2], ssv[ti % 2], rsv[ti % 2]
                P.dma("sp", "xt%d" % (ti % 2), XTp, xsrc[tg * 128:(tg + 1) * 128, :], rk=[("x", tg)], w=[XTp])
                act(HBp, XTp, AF.Square, accum=ssp)
                ts("dve", rsp, ssp, 1.0 / DM, EPS, ALU.mult, ALU.add)
                act(rsp, rsp, AF.Sqrt)
                recip(rsp, rsp)
                stt(XTp, XTp, rsp, Gt, ALU.mult, ALU.mult)
                tt("dve", HBp[:, 0:1024], XTp[:, 0:1024], SHt[:, 0:1024], ALU.add)
                tt("pool", HBp[:, 1024:2048], XTp[:, 1024:2048], SHt[:, 1024:2048], ALU.add)

            def back(ti):
                HBp = HBs[ti % 2]
                for half in range(2):
                    pb = bankb(4 + half + 2 * (ti % 2))
                    for i in range(8):
                        kc = half * 8 + i
                        tr(pb[:, i * 128:(i + 1) * 128], HBp[:, kc * 128:(kc + 1) * 128], ident)
                    dst = hT[:, half * 8:(half + 1) * 8, ti * 128:(ti + 1) * 128]
                    cp("act" if half == 0 else "dve", dst, pb.rearrange("p (a b) -> p a b", a=8))

            front(0)
            for ti in range(8):
                if ti + 1 < 8:
                    front(ti + 1)
                back(ti)

        def phase_D(l, tiles, ysrc, last, GAT, final_blk, FG):
            wout = ba(O_BIG, 16384).rearrange("p (k n) -> p k n", k=16)
            for q4 in range(4):
                wv = wout[:, q4 * 4:(q4 + 1) * 4, :]
                P.dma("pool", "wout", wv, w_out[l, q4 * 512:(q4 + 1) * 512, :].rearrange("(k p) n -> p k n", p=128), w=[wv])
            P.dma("sp", "gat", GAT, modb[l, 2], rk=[("modb", l, 2, c) for c in range(4)], w=[GAT])
            if last and not raw:
                P.dma("sp", "fg", FG, fin_g, w=[FG])
            elif final_blk and not last:
                load_mod(l + 1)
            for it, tg in enumerate(tiles):
                b0 = 4 * (it % 2)
                for kc in range(16):
                    lhsT = ysrc(kc, tg)
                    for n in range(4):
                        mm(bank(b0 + n), lhsT, wout[:, kc, n * 512:(n + 1) * 512], kc == 0, kc == 15)
                P.dma("sp", "xt0", XT, (x_in if l == 0 else xres)[tg * 128:(tg + 1) * 128, :], rk=[("x", tg)], w=[XT])
                for n in range(4):
                    tmp = TMP[n]
                    tt("dve", tmp, bank(b0 + n), GAT[:, n * 512:(n + 1) * 512], ALU.mult)
                    tt("pool", XT[:, n * 512:(n + 1) * 512], XT[:, n * 512:(n + 1) * 512], tmp, ALU.add)
                if last and not raw:
                    act(HB, XT, AF.Square, accum=ss)
                    ts("dve", rstd, ss, 1.0 / DM, EPS, ALU.mult, ALU.add)
                    act(rstd, rstd, AF.Sqrt)
                    recip(rstd, rstd)
                    stt(XT, XT, rstd, FG, ALU.mult, ALU.mult)
                if last:
                    P.dma("sp", "st_out", out_d[tg * 128:(tg + 1) * 128, :], XT, r=[XT], wk=[("out", tg)])
                else:
                    P.dma("sp", "st_x", xres[tg * 128:(tg + 1) * 128, :], XT, r=[XT], wk=[("x", tg)])

        def even_layer(l, last):
            i2 = l // 2
            hT = ba(O_BIG, 8192).rearrange("p (k t) -> p k t", k=16)
            mT = ba(O_RA, 8200).rearrange("p (f t) -> p f t", f=8)
            ybT = ba(O_RB, 8192).rearrange("p (f t) -> p f t", f=8)
            yaT = ba(O_RA, 8192).rearrange("p (f t) -> p f t", f=8)
            cosT, sinT = CSa, CSb
            P.dma("sp", "cs", cosT, cos_d, w=[cosT])
            P.dma("sp", "cs", sinT, sin_d, w=[sinT])
            if l == 0:
                load_mod(l)
            def issue_exchange():
                P.dma("sp", "st_cin", cin_m[0, :].rearrange("(f p) -> p f", p=128), mT[:, :, 1], r=[mT[:, :, 1]], wk=[("cm", 0)],
                      allow_slow_non_contiguous=True)
                P.dma("sp", "st_cin", cin_m[1, :].rearrange("(f p) -> p f", p=128), mT[:, :, 2048], r=[mT[:, :, 2048]], wk=[("cm", 1)],
                      allow_slow_non_contiguous=True)

                def coll(src, dst, rk, wk):
                    P.async_op("pool", "cc", lambda e: e.collective_compute("AllGather", ALU.bypass, replica_groups=RG, ins=[src], outs=[dst]),
                               inc=1, rk=rk, wk=wk)
                coll(cin_m, cout_m, [("cm", 0), ("cm", 1)], ["cout_m"])
                for g in range(4):
                    coll(cin_k[g], cout_k[g], [("ck", h, b) for h in (2 * g, 2 * g + 1) for b in range(2)], [("cout_k", g)])
                for g in range(4):
                    coll(cin_v[g], cout_v[g], [("cv2", tg, j) for tg in range(4 * g, 4 * g + 4) for j in range(2)], [("cout_v", g)])

            wcnt = 0
            stg_i = 0
            pcnt = 0
            for blk in range(2):
                phase_A(l, blk, hT)
                dump_and_stop("A", lambda: [P.dma("pool", "st_out", out_d[kc * 128:(kc + 1) * 128, 0:1024], hT[:, kc, :], r=[hT[:, kc, :]], wk=[("out", kc)]) for kc in range(16)])
                t0 = blk * 1024
                for gi, (kind, j) in enumerate(EGROUPS):
                    Wv = Wslot[wcnt % 2]
                    P.dma("pool", "W%d" % (wcnt % 2), Wv,
                          w_in_e[i2, :, gi * 512:(gi + 1) * 512].rearrange("(k p) n -> p k n", p=128), w=[Wv])
                    wcnt += 1

                    def fm(i, tb):
                        nonlocal pcnt
                        ps = bank(pcnt % 4)
                        pcnt += 1
                        for kc in range(16):
                            mm(ps, Wv[:, kc, i * 128:(i + 1) * 128], hT[:, kc, tb * 512:(tb + 1) * 512], kc == 0, kc == 15)
                        return ps

                    if kind == "m":
                        for tb in range(2):
                            for fl in range(2):
                                ft = 2 * j + fl
                                pa = fm(fl, tb)
                                pb_ = fm(2 + fl, tb)
                                tmp = TMP[pcnt % 4]
                                cp("act", tmp, pa)
                                tt("dve", mT[:, ft, 1 + t0 + tb * 512:1 + t0 + (tb + 1) * 512], pb_, tmp, ALU.mult)
                    elif kind in ("k", "q"):
                        for i in range(4):
                            h = 4 * j + i
                            st = STG[stg_i % 4]
                            stg_i += 1
                            for tb in range(2):
                                ps = fm(i, tb)
                                tok = slice(t0 + tb * 512, t0 + (tb + 1) * 512)
                                qs = ba(O_TMP + 512 * (pcnt % 2), 256)
                                qc = TMP[2 + pcnt % 2]
                                tt("dve", qs, ps, sinT[:, tok], ALU.mult)
                                tt("dve", qc, ps, cosT[:, tok], ALU.mult)
                                ps2 = bank(6 + pcnt % 2)
                                mm(ps2, pswap, qs, True, True)
                                tt("dve", st[:, tb * 512:(tb + 1) * 512], ps2, qc, ALU.add)
                            if kind == "k":
                                P.dma("sp", "st_cin", cin_k[h // 2][(h % 2) * 128:(h % 2 + 1) * 128, t0:t0 + 1024], st, r=[st], wk=[("ck", h, blk)])
                            else:
                                P.dma("sp", "st_q", qT_d[h * 128:(h + 1) * 128, t0:t0 + 1024], st, r=[st], wk=[("q", h, blk)])
                    elif kind == "v":
                        for ti in range(8):
                            tg = blk * 8 + ti
                            ps = bank(pcnt % 4)
                            pcnt += 1
                            for kc in range(16):
                                mm(ps, hT[:, kc, ti * 128:(ti + 1) * 128], Wv[:, kc, :], kc == 0, kc == 15)
                            st = STG[stg_i % 4][:, 0:512]
                            stg_i += 1
                            cp("act", st, ps)
                            P.dma("sp", "st_cin", v_own[tg * 128:(tg + 1) * 128, j * 512:(j + 1) * 512], st, r=[st], wk=[("cv", tg, j)])
                            P.dma("sp", "st_cin", cin_v[tg // 4][(tg % 4) * 128:(tg % 4 + 1) * 128, j * 512:(j + 1) * 512], st, r=[st], wk=[("cv2", tg, j)])
                        if blk == 1 and j == 1:
                            issue_exchange()
                    elif kind == "z":
                        for i in range(4):
                            h = 4 * j + i
                            st = STG[stg_i % 4]
                            stg_i += 1
                            for tb in range(2):
                                ps = fm(i, tb)
                                act(st[:, tb * 512:(tb + 1) * 512], ps, AF.Silu)
                            P.dma("sp", "st_sz", szT_d[h * 128:(h + 1) * 128, t0:t0 + 1024], st, r=[st], wk=[("sz", h, blk)])
                    elif kind == "c":
                        for tb in range(2):
                            for fl in range(2):
                                ft = 2 * j + fl
                                pa = fm(fl, tb)
                                pb_ = fm(2 + fl, tb)
                                tmp = TMP[pcnt % 4]
                                act(tmp, pb_, AF.Silu)
                                tt("dve", ybT[:, ft, t0 + tb * 512:t0 + (tb + 1) * 512], pa, tmp, ALU.mult)
            dump_and_stop("B", lambda: [P.dma("pool", "st_out", out_d[ft * 128:(ft + 1) * 128, 0:2048], mT[:, ft, 1:2049], r=[mT[:, ft, :]], wk=[("out", ft)]) for ft in range(8)]
                          + [P.dma("pool", "st_out", out_d[1024 + ft * 128:1024 + (ft + 1) * 128, 0:2048], ybT[:, ft, :], r=[ybT[:, ft, :]], wk=[("out", 8 + ft)]) for ft in range(8)])
            P.dma("sp", "vh", vL_d[0:512, :], cout_v[2][0:512, :], rk=[("cout_v", 2)], wk=["vL"])
            P.dma("sp", "vh", vL_d[512:1024, :], cout_v[3][0:512, :], rk=[("cout_v", 3)], wk=["vL"])
            P.dma("sp", "vh", vR_d[0:512, :], cout_v[0][512:1024, :], rk=[("cout_v", 0)], wk=["vR"])
            P.dma("sp", "vh", vR_d[512:1024, :], cout_v[1][512:1024, :], rk=[("cout_v", 1)], wk=["vR"])
            P.dma("sp", "mh", mT[:, :, 0], cout_m[1, :].rearrange("(f p) -> p f", p=128), rk=["cout_m"], w=[mT[:, :, 0]],
                  allow_slow_non_contiguous=True)
            P.dma("sp", "mh", mT[:, :, 2049], cout_m[2, :].rearrange("(f p) -> p f", p=128), rk=["cout_m"], w=[mT[:, :, 2049]],
                  allow_slow_non_contiguous=True)
            ts("dve", mT[:, :, 0], mT[:, :, 0], VLR[:, 0:1], None, ALU.mult)
            ts("dve", mT[:, :, 2049], mT[:, :, 2049], VLR[:, 1:2], None, ALU.mult)
            dump_and_stop("X", lambda: [P.dma("pool", "st_out", out_d[ft * 128:(ft + 1) * 128, 0:2048], mT[:, ft, 0:2048], r=[mT[:, ft, :]], wk=[("out", ft)]) for ft in range(8)]
                          + [P.dma("pool", "st_out", out_d[1024:1536, 0:2048], cout_k[0], rk=[("cout_k", 0)], wk=[("out", 8)])])
            dump_and_stop("C", lambda: [P.dma("pool", "st_out", out_d[ft * 128:(ft + 1) * 128, 0:2048], ybT[:, ft, :], r=[ybT[:, ft, :]], wk=[("out", ft)]) for ft in range(8)])
            Vs = [ba(O_BIG + 4416 * s_, 4416).rearrange("p (t c) -> p t c", c=128) for s_ in range(2)]
            PT = [ba(O_BIG + 8832 + 128 * s, 128) for s in range(4)]
            qks = [O_BIG + 9344, O_CS]
            Qs = [ba(o, 1024) for o in qks]
            Ks = [ba(o + 1024, 2048) for o in qks]
            SZs = [ba(o + 3072, 1024) for o in qks]
            acc = fa(O_GSH, 2048)
            den = fa(O_GSH + 2048, 2048)
            PAT = [(1, 0), (4, 17), (16, 37)]
            scnt = 0
            gcnt = 0

            def load_qk(h):
                Qb, Kb, SZb = Qs[h % 2], Ks[h % 2], SZs[h % 2]
                hr = (h % 2) * 128
                P.dma(QENG, "qld%d" % (h % 2), Qb, qT_d[h * 128:(h + 1) * 128, :], rk=[("q", h, 0), ("q", h, 1)], w=[Qb])
                P.dma(QENG, "qld%d" % (h % 2), SZb, szT_d[h * 128:(h + 1) * 128, :], rk=[("sz", h, 0), ("sz", h, 1)], w=[SZb])
                P.dma(QENG, "qld%d" % (h % 2), Kb[:, 0:1024], cout_k[h // 2][hr:hr + 128, 1024:2048], rk=[("cout_k", h // 2)], w=[Kb[:, 0:1024]])
                P.dma(QENG, "qld%d" % (h % 2), Kb[:, 1024:3072], cin_k[h // 2][hr:hr + 128, :], rk=[("ck", h, 0), ("ck", h, 1)], w=[Kb[:, 1024:3072]])
                P.dma(QENG, "qld%d" % (h % 2), Kb[:, 3072:4096], cout_k[h // 2][256 + hr:256 + hr + 128, 0:1024], rk=[("cout_k", h // 2)], w=[Kb[:, 3072:4096]])

            def load_v(h):
                Vb = Vs[h % 2]
                cvk = [("cv", tg, h // 4) for tg in range(16)]
                c0 = h * 128
                for D, base in PAT:
                    Lq = T // D
                    NJ = Lq // 128 + 1
                    Vp = Vb[:, base:base + D * NJ, :].rearrange("p (r j) c -> p r j c", r=D)
                    own = v_own[:, c0:c0 + 128].rearrange("(j a r) c -> a r j c", a=128, r=D)
                    if NJ == 2:
                        P.dma("sp", "vld%d" % (h % 2), Vp[64:128, :, 0, :], own[0:64, :, 0, :], rk=cvk, w=[Vp[64:128, :, 0, :]])
                        P.dma("sp", "vld%d" % (h % 2), Vp[0:64, :, 1, :], own[64:128, :, 0, :], rk=cvk, w=[Vp[0:64, :, 1, :]])
                    else:
                        for r in range(D):
                            P.dma("sp", "vld%d" % (h % 2), Vp[64:128, r, 0:NJ - 1, :], own[0:64, r], rk=cvk, w=[Vp[64:128, r, 0:NJ - 1, :]])
                            P.dma("sp", "vld%d" % (h % 2), Vp[0:64, r, 1:NJ, :], own[64:128, r], rk=cvk, w=[Vp[0:64, r, 1:NJ, :]])
                    lsrc = vL_d[1024 - 64 * D:1024, c0:c0 + 128].rearrange("(a r) c -> a r c", r=D)
                    P.dma("sp", "vld%d" % (h % 2), Vp[0:64, :, 0, :], lsrc, rk=["vL"], w=[Vp[0:64, :, 0, :]])
                    rsrc = vR_d[0:64 * D, c0:c0 + 128].rearrange("(a r) c -> a r c", r=D)
                    P.dma("sp", "vld%d" % (h % 2), Vp[64:128, :, NJ - 1, :], rsrc, rk=["vR"], w=[Vp[64:128, :, NJ - 1, :]])

            def conv_ft(ft):
                cw = CW[:, i2 * 24 + ft * 3:i2 * 24 + ft * 3 + 3]
                ts("pool", XT, mT[:, ft, 0:2048], cw[:, 0:1], None, ALU.mult)
                stt(XT, mT[:, ft, 1:2049], cw[:, 1:2], XT, ALU.mult, ALU.add)
                stt(XT, mT[:, ft, 2:2050], cw[:, 2:3], XT, ALU.mult, ALU.add)
                tt("pool", ybT[:, ft, :], XT, ybT[:, ft, :], ALU.mult)

            load_qk(0)
            load_v(0)
            conv_ft(0)
            conv_ft(1)
            for hp in range(4):
                for hh in range(2):
                    h = 2 * hp + hh
                    Qb, Kb, SZb = Qs[h % 2], Ks[h % 2], SZs[h % 2]
                    Vb = Vs[h % 2]
                    if h + 1 < NH:
                        load_qk(h + 1)
                        load_v(h + 1)
                    blocks = []
                    gviews = {}
                    for pi, (D, base) in enumerate(PAT):
                        NB = (T // D) // 128
                        NJ = NB + 1
                        for g in range(4):
                            if D == 1:
                                grp = [(0, 4 * g + b) for b in range(4)]
                                gviews[(pi, g)] = (acc[:, 512 * g:512 * (g + 1)], den[:, 512 * g:512 * (g + 1)])
                            elif D == 4:
                                grp = [(g, b) for b in range(4)]
                                gviews[(pi, g)] = (acc[:, g:2048:4], den[:, g:2048:4])
                            else:
                                grp = [(4 * g + b, 0) for b in range(4)]
                                gviews[(pi, g)] = (acc.rearrange("p (c s) -> p s c", s=16)[:, 4 * g:4 * g + 4, :],
                                                   den.rearrange("p (c s) -> p s c", s=16)[:, 4 * g:4 * g + 4, :])
                            for b, (r, i) in enumerate(grp):
                                blocks.append((pi, D, base, g, b, r, i, NB, NJ))
                    slots = {}
                    gbanks = {}

                    def s_stage(n):
                        nonlocal scnt
                        pi, D, base, g, b, r, i, NB, NJ = blocks[n]
                        sl = scnt % 4
                        scnt += 1
                        slots[n] = sl
                        psc = bank(sl)[:, 0:256]
                        q0 = r + D * 128 * i
                        qc_ = Qb[:, q0:q0 + 127 * D + 1:D]
                        kind = (1 if i == 0 else 0) + (2 if i == NB - 1 else 0)
                        if MASKMM:
                            for jj in range(2):
                                ks = 1024 + r + D * (128 * (i + jj) - 64)
                                mm(psc[:, 128 * jj:128 * (jj + 1)], Kb[:, ks:ks + 127 * D + 1:D], qc_, True, False)
                                mm(psc[:, 128 * jj:128 * (jj + 1)], ident, masks[kind][:, 128 * jj:128 * (jj + 1)], False, True)
                            act(PT[sl], psc, AF.Exp, scale=SCALE)
                        else:
                            for jj in range(2):
                                ks = 1024 + r + D * (128 * (i + jj) - 64)
                                mm(psc[:, 128 * jj:128 * (jj + 1)], Kb[:, ks:ks + 127 * D + 1:D], qc_, True, True)
                            act(PT[sl], psc, AF.Exp, scale=SCALE)
                            tt("pool", PT[sl], PT[sl], masks[kind], ALU.mult)

                    def p_stage(n):
                        nonlocal gcnt
                        pi, D, base, g, b, r, i, NB, NJ = blocks[n]
                        if b == 0:
                            gbanks[(pi, g)] = (bank(4 + gcnt % 2), bank(6 + gcnt % 2))
                            gcnt += 1
                        po, pd = gbanks[(pi, g)]
                        pt = PT[slots[n]]
                        for jj in range(2):
                            vt = Vb[:, base + r * NJ + i + jj, :]
                            mm(po[:, 128 * b:128 * (b + 1)], vt, pt[:, 128 * jj:128 * (jj + 1)], jj == 0, jj == 1)
                        for jj in range(2):
                            mm(pd[:, 128 * b:128 * (b + 1)], ones, pt[:, 128 * jj:128 * (jj + 1)], jj == 0, jj == 1)
                        if b == 3:
                            if D == 16:
                                pov = po.rearrange("p (b c) -> p b c", b=4)
                                pdv = pd.rearrange("p (b c) -> p b c", b=4)
                            else:
                                pov, pdv = po, pd
                            va, vd = gviews[(pi, g)]
                            if pi == 0:
                                cp("dve", va, pov)
                                cp("dve", vd, pdv)
                            else:
                                tt("dve", va, va, pov, ALU.add)
                                tt("dve", vd, vd, pdv, ALU.add)

                    LOOK = int(os.environ.get('ATT_LOOK', '2'))
                    nb = len(blocks)
                    for n in range(min(LOOK, nb)):
                        s_stage(n)
                    for n in range(nb):
                        if LOOK == 0:
                            s_stage(n)
                        elif n + LOOK < nb:
                            s_stage(n + LOOK)
                        p_stage(n)
                    if h + 2 < 8:
                        conv_ft(h + 2)
                    if LNEXP:
                        act(den, den, AF.Ln)
                        act(den, den, AF.Exp, scale=-1.0)
                    else:
                        recip(den, den)
                    tt("dve", acc, acc, den, ALU.mult)
                    tt("pool", yaT[:, h, :], acc, SZb, ALU.mult)

            dump_and_stop("T", lambda: [P.dma("pool", "st_out", out_d[ft * 128:(ft + 1) * 128, 0:2048], yaT[:, ft, :], r=[yaT[:, ft, :]], wk=[("out", ft)]) for ft in range(8)]
                          + [P.dma("pool", "st_out", out_d[1024 + ft * 128:1024 + (ft + 1) * 128, 0:2048], ybT[:, ft, :], r=[ybT[:, ft, :]], wk=[("out", 8 + ft)]) for ft in range(8)])

            def ysrc(kc, tg):
                if kc < 8:
                    return yaT[:, kc, tg * 128:(tg + 1) * 128]
                return ybT[:, kc - 8, tg * 128:(tg + 1) * 128]
            phase_D(l, list(range(16)), ysrc, last, CSa, True, CSb)

        def odd_layer(l, last):
            i2 = l // 2
            hT = ba(O_BIG, 8192).rearrange("p (k t) -> p k t", k=16)
            yT = ba(O_RA, 8192).rearrange("p (k t) -> p k t", k=16)
            gv = ba(O_RB, 8192).rearrange("p (t f) -> p t f", t=8)
            LNG, LNB = CSa, CSb
            P.dma("sp", "cs", LNG, lng_d[i2], w=[LNG])
            P.dma("sp", "cs", LNB, lnb_d[i2], w=[LNB])
            P.dma("pool", "wst", WST, wsT_d[i2], w=[WST])
            wcnt = 0
            pcnt = 0
            s1 = STAT[:, 8:40].rearrange("p (t n) -> p t n", n=4)
            s2 = STAT[:, 40:48]
            mean = STAT[:, 48:56]
            var = STAT[:, 56:64]
            for blk in range(2):
                phase_A(l, blk, hT)
                for n in range(4):
                    Wv = Wslot[wcnt % 2]
                    P.dma("pool", "W%d" % (wcnt % 2), Wv,
                          w_in_o[i2, :, 2048 + n * 512:2048 + (n + 1) * 512].rearrange("(k p) n -> p k n", p=128), w=[Wv])
                    wcnt += 1
                    for ti in range(8):
                        ps = bank(pcnt % 4)
                        pcnt += 1
                        for kc in range(16):
                            mm(ps, hT[:, kc, ti * 128:(ti + 1) * 128], Wv[:, kc, :], kc == 0, kc == 15)
                        act(gv[:, ti, n * 512:(n + 1) * 512], ps, AF.Gelu_apprx_tanh, accum=s1[:, ti, n:n + 1])
                Wu_next = Wslot[wcnt % 2]
                P.dma("pool", "W%d" % (wcnt % 2), Wu_next,
                      w_in_o[i2, :, 0:512].rearrange("(k p) n -> p k n", p=128), w=[Wu_next])
                wcnt += 1
                for ti in range(8):
                    act(HBs[ti % 2], gv[:, ti, :], AF.Square, accum=s2[:, ti:ti + 1])
                P.op("dve", lambda e: e.tensor_reduce(out=mean, in_=s1, axis=mybir.AxisListType.X, op=ALU.add), r=[s1], w=[mean])
                ts("dve", mean, mean, 1.0 / DM, None, ALU.mult)
                tt("dve", var, mean, mean, ALU.mult)
                stt(var, s2, 1.0 / DM, var, ALU.mult, ALU.subtract)
                ts("dve", var, var, EPS, None, ALU.add)
                act(var, var, AF.Sqrt)
                recip(var, var)
                nmr = STAT[:, 64:72]
                stt(nmr, mean, -1.0, var, ALU.mult, ALU.mult)
                for ti in range(8):
                    XL = XTs[ti % 2]
                    act(XL, gv[:, ti, :], AF.Identity, scale=var[:, ti:ti + 1], bias=nmr[:, ti:ti + 1])
                    tt("dve", XL, XL, LNG, ALU.mult)
                    tt("dve", gv[:, ti, 0:1024], XL[:, 0:1024], LNB[:, 0:1024], ALU.add)
                    tt("pool", gv[:, ti, 1024:2048], XL[:, 1024:2048], LNB[:, 1024:2048], ALU.add)
                gu_all = XT.bitcast(BF16).rearrange("p (t f) -> p t f", t=8)
                for c in range(4):
                    Wu = Wu_next
                    Wz = Wslot[wcnt % 2]
                    P.dma("pool", "W%d" % (wcnt % 2), Wz,
                          w_in_o[i2, :, 4096 + c * 512:4096 + (c + 1) * 512].rearrange("(k p) n -> p k n", p=128), w=[Wz])
                    wcnt += 1
                    for ti in range(8):
                        pu = bank(pcnt % 4)
                        pcnt += 1
                        for kc in range(16):
                            mm(pu, hT[:, kc, ti * 128:(ti + 1) * 128], Wu[:, kc, :], kc == 0, kc == 15)
                        act(gu_all[:, ti, :], pu, AF.Gelu_apprx_tanh)
                    if c + 1 < 4:
                        Wu_next = Wslot[wcnt % 2]
                        P.dma("pool", "W%d" % (wcnt % 2), Wu_next,
                              w_in_o[i2, :, (c + 1) * 512:(c + 2) * 512].rearrange("(k p) n -> p k n", p=128), w=[Wu_next])
                        wcnt += 1
                    pend = None
                    for ti in range(8):
                        pz = bank(pcnt % 4)
                        pcnt += 1
                        for kc in range(16):
                            mm(pz, hT[:, kc, ti * 128:(ti + 1) * 128], Wz[:, kc, :], kc == 0, kc == 15)
                        pm = bank(6 + ti % 2)
                        for gg in range(2):
                            g = 2 * c + gg
                            mm(pm[:, gg * 256:(gg + 1) * 256], WST[:, g, :], gv[:, ti, g * 256:(g + 1) * 256], True, True)
                        if pend is not None:
                            pend()
                        sz = TMP[(2 * ti) % 4]
                        tm = TMP[(2 * ti + 1) % 4]
                        act(sz, pz, AF.Silu)
                        for gg in range(2):
                            g = 2 * c + gg
                            stt(tm[:, gg * 256:(gg + 1) * 256], pm[:, gg * 256:(gg + 1) * 256], BSt[:, i2 * 8 + g:i2 * 8 + g + 1],
                                gu_all[:, ti, gg * 256:(gg + 1) * 256], ALU.add, ALU.mult, xr=[pm])
                        yb = HB[:, (ti % 2) * 512:(ti % 2) * 512 + 512]
                        tt("pool", yb, tm, sz, ALU.mult)

                        def pend(ti=ti, yb=yb, c=c):
                            pb = bankb(4 + ti % 2)
                            for i in range(4):
                                tr(pb[:, i * 128:(i + 1) * 128], yb[:, i * 128:(i + 1) * 128], ident)
                            cp("act" if ti % 2 == 0 else "dve", yT[:, 4 * c:4 * c + 4, ti * 128:(ti + 1) * 128],
                               pb[:, 0:512].rearrange("p (a b) -> p a b", a=4))
                    pend()

                def ysrc(kc, tg, blk=blk):
                    tl = tg - blk * 8
                    return yT[:, kc, tl * 128:(tl + 1) * 128]
                phase_D(l, [blk * 8 + ti for ti in range(8)], ysrc, last, fa(O_RB, 2048), blk == 1, fa(O_RB + 2048, 2048))

        try:
            dump_and_stop("mod", lambda: [P.dma("sp", "st_out", out_d[j * 128:(j + 1) * 128, :], modb[0, j], rk=[("modb", 0, j, c) for c in range(4)], wk=[("out", j)]) for j in range(3)])
            for l in range(nlayers):
                last = (l == nlayers - 1)
                if l % 2 == 0:
                    even_layer(l, last)
                else:
                    odd_layer(l, last)
        except _Stop:
            pass
        P.emit(final_wait_grps=["st_out"])
    return nc


def make_inputs(inputs, nlayers=4):
    f32 = np.float32
    x = np.asarray(inputs["x"], f32)
    c = np.asarray(inputs["c"], f32)
    w_mod = np.stack([np.asarray(inputs["ab_w_mod"][0]), np.asarray(inputs["sg_w_mod"][0]),
                      np.asarray(inputs["ab_w_mod"][1]), np.asarray(inputs["sg_w_mod"][1])]).astype(f32)
    b_mod4 = np.stack([inputs["ab_b_mod"][0], inputs["sg_b_mod"][0], inputs["ab_b_mod"][1], inputs["sg_b_mod"][1]]).astype(f32)
    b_mod = np.ascontiguousarray(np.broadcast_to(b_mod4[:, None, :], (4, 128, 3 * DM)))
    ng4 = np.stack([inputs["ab_norm_g"][0], inputs["sg_norm_g"][0], inputs["ab_norm_g"][1], inputs["sg_norm_g"][1]]).astype(f32)
    norm_g = np.ascontiguousarray(np.broadcast_to(ng4[:, None, :], (4, 128, DM)))
    fin_g = np.ascontiguousarray(np.broadcast_to(np.asarray(inputs["final_norm_g"], f32)[None, :], (128, DM)))
    perm = even_perm()
    w_in_e = np.ascontiguousarray(np.asarray(inputs["ab_w_in"], f32)[:, :, perm])
    w_in_o = np.ascontiguousarray(np.asarray(inputs["sg_w_in"], f32))
    w_out = np.stack([inputs["ab_w_out"][0], inputs["sg_w_out"][0], inputs["ab_w_out"][1], inputs["sg_w_out"][1]]).astype(f32)
    cw = np.asarray(inputs["ab_conv_w"], f32)
    convw = np.ascontiguousarray(cw.reshape(2, 3, 8, 128).transpose(0, 3, 2, 1).reshape(2, 128, 24))
    wsT = np.ascontiguousarray(np.asarray(inputs["sg_w_s"], f32).transpose(0, 3, 1, 2))
    bs = np.ascontiguousarray(np.asarray(inputs["sg_b_s"], f32).transpose(0, 2, 1))
    lng = np.ascontiguousarray(np.broadcast_to(np.asarray(inputs["sg_ln_g"], f32)[:, None, :], (2, 128, DM)))
    lnb = np.ascontiguousarray(np.broadcast_to(np.asarray(inputs["sg_ln_b"], f32)[:, None, :], (2, 128, DM)))
    inv = (f32(10000.0) ** (-(np.arange(64, dtype=f32) / f32(64)))).astype(f32)
    invp = inv[np.arange(128) % 64]
    a = np.arange(128)[:, None]
    cc = np.arange(128)[None, :]
    lo = (cc <= a).astype(f32)
    up = (cc >= a).astype(f32)
    psw = np.zeros((128, 128), f32)
    for m in range(128):
        if m < 64:
            psw[m + 64, m] = -1.0
        else:
            psw[m - 64, m] = 1.0
    maps = []
    for core in range(8):
        b, half = core // 2, core % 2
        vL, vR = (1.0, 0.0) if half == 1 else (0.0, 1.0)
        pos = (half * T + np.arange(T)).astype(f32)
        ang = (pos[None, :] * invp[:, None]).astype(f32)
        lo_f = lo.copy()
        lo_f[:64, :] *= vL
        up_l = up.copy()
        up_l[64:, :] *= vR
        NEG = f32(-30000.0)
        mk = [(np.where(m_ > 0.5, f32(0.0), NEG) if MASKMM else m_) for m_ in
              (np.concatenate([lo, up], 1), np.concatenate([lo_f, up], 1),
               np.concatenate([lo, up_l], 1), np.concatenate([lo_f, up_l], 1))]
        cbf = np.concatenate([np.eye(128, dtype=f32), np.ones((128, 128), f32), psw] + mk, 1).astype(ml_dtypes.bfloat16)
        csil = np.ascontiguousarray(c.reshape(4, 16, 128).transpose(2, 1, 0))
        sel = np.zeros((4, 128), f32)
        sel[b, :] = 1.0
        maps.append({
            "x": np.ascontiguousarray(x[b, half * T:(half + 1) * T, :]),
            "csil": csil, "sel": sel, "w_mod": np.ascontiguousarray(w_mod[:nlayers, :, half * 3072:(half + 1) * 3072]), "b_mod": b_mod, "norm_g": norm_g, "fin_g": fin_g,
            "w_in_e": w_in_e[:max(1, (nlayers + 1) // 2)], "w_in_o": w_in_o[:max(1, nlayers // 2)], "w_out": w_out[:nlayers], "convw": convw,
            "cos": np.cos(ang).astype(f32), "sin": np.sin(ang).astype(f32), "cbf": cbf,
            "vlr": np.ascontiguousarray(np.broadcast_to(np.array([vL, vR], f32)[None, :], (128, 2))),
            "wsT": wsT, "bs": bs, "lng": lng, "lnb": lnb,
        })
    return maps


_NC_CACHE = {}


def run(inputs, nlayers=4, raw=False, stop=None):
    if (nlayers, raw, stop) not in _NC_CACHE:
        _NC_CACHE[(nlayers, raw, stop)] = build_nc(nlayers, raw, stop)
    nc = _NC_CACHE[(nlayers, raw, stop)]
    maps = make_inputs(inputs, nlayers)
    res = run_bass_kernel_spmd(nc, maps, core_ids=list(range(8)))
    out = np.empty((4, 4096, DM), np.float32)
    for core in range(8):
        b, half = core // 2, core % 2
        out[b, half * T:(half + 1) * T, :] = res.results[core]["out"]
    return out


def kernel(**inputs):
    return run(inputs, 4)
```
